# Optimizing a Trainium2 kernel written in Bass

```python
import math
import jax, jax.numpy as jnp
from jax import lax
import numpy as np

D_MODEL = 1024
BATCH = 8
SEQ = 4096
DEPTH = 1
DEC_BATCH = 32
DEC_SEQ = 16
PAST_LEN = 2048

CHUNK = 64
Q_BLOCK = 128
H_G = 4
DK_G = 64
DV_G = 128
GATE_RANK = 16
GATE_TAU = 16.0
H_D = 4
DH_D = 64
DV_D = 2 * DH_D
D_FF = 2816
EPS = 1e-6
IN_SPLITS = (H_G * DK_G, H_G * DK_G, H_G * DV_G, GATE_RANK, H_G * DV_G,
             H_D * 2 * DH_D, H_D * 2 * DH_D, H_D * DV_D)
D_IN = H_G * DK_G * 2 + H_G * DV_G * 2 + GATE_RANK + H_D * 2 * DH_D * 2 + H_D * DV_D
MIX_WIDTH = H_G * DV_G + H_D * DV_D

kernel_name = "hybrid_gla_diffattn_streaming_step"


def rmsnorm(x, g):
    xf = x.astype(jnp.float32)
    r = lax.rsqrt(jnp.mean(xf * xf, axis=-1, keepdims=True) + EPS)
    return (xf * r * g.astype(jnp.float32)).astype(x.dtype)


def ffn_half(x, pre_g, post_g, w_gate, w_up, w_down):
    h = rmsnorm(x, pre_g)
    f = (jax.nn.silu(h @ w_gate) * (h @ w_up)) @ w_down
    return x + 0.5 * rmsnorm(f, post_g)


def split_in(z):
    offs, acc = [], 0
    for s in IN_SPLITS[:-1]:
        acc += s
        offs.append(acc)
    return jnp.split(z, offs, axis=-1)


def project(h, w_in, w_a2, b_a):
    B, T, _ = h.shape
    gq, gk, gv, gr, gg, dq, dk, dv = split_in(h @ w_in)
    to_bhtd = lambda t, d: t.reshape(B, T, -1, d).transpose(0, 2, 1, 3)
    gq = to_bhtd(gq, DK_G) * (DK_G ** -0.5)
    gk = to_bhtd(gk, DK_G)
    gv = to_bhtd(gv, DV_G)
    la = jax.nn.log_sigmoid((gr @ w_a2 + b_a).astype(jnp.float32)) / GATE_TAU
    la = to_bhtd(la, DK_G)
    dq = dq.reshape(B, T, H_D, 2, DH_D)
    dk = dk.reshape(B, T, H_D, 2, DH_D)
    dv = dv.reshape(B, T, H_D, DV_D)
    return gq, gk, gv, la, gg, dq, dk, dv


def gla_block(S0, q, k, v, la):
    L = q.shape[2]
    S0 = S0.astype(jnp.float32)
    b = jnp.cumsum(la, axis=2)
    causal = jnp.tril(jnp.ones((L, L), dtype=bool))
    decay = jnp.exp(jnp.where(causal[:, :, None],
                              b[:, :, :, None, :] - b[:, :, None, :, :], -jnp.inf))
    A = jnp.einsum('bhtd,bhsd,bhtsd->bhts', q, k, decay)
    o = jnp.einsum('bhts,bhsv->bhtv', A, v) + jnp.einsum('bhtd,bhdv->bhtv', q * jnp.exp(b), S0)
    bL = b[:, :, -1:, :]
    S = jnp.exp(bL[:, :, 0, :])[..., None] * S0 + jnp.einsum('bhsd,bhsv->bhdv', k * jnp.exp(bL - b), v)
    return S, o.astype(v.dtype)


def gla_prompt(q, k, v, la):
    B, H, S, _ = q.shape
    nc = S // CHUNK
    to_chunks = lambda t: t.reshape(B, H, nc, CHUNK, t.shape[-1]).transpose(2, 0, 1, 3, 4)
    S0 = jnp.zeros((B, H, DK_G, DV_G), jnp.float32)
    Sf, o = lax.scan(lambda St, xs: gla_block(St, *xs), S0,
                     (to_chunks(q), to_chunks(k), to_chunks(v), to_chunks(la)))
    o = o.transpose(1, 2, 0, 3, 4).reshape(B, H, S, DV_G)
    return o, Sf


def diff_attend(q, k, v, mask, lam, g, lam_init):
    s = jnp.einsum('bqhcd,bkhcd->bhcqk', q, k).astype(jnp.float32) * (DH_D ** -0.5)
    if mask is not None:
        s = jnp.where(mask, s, -jnp.inf)
    a = jax.nn.softmax(s, axis=-1)
    w = a[:, :, 0] - lam * a[:, :, 1]
    o = jnp.einsum('bhqk,bkhv->bqhv', w.astype(v.dtype), v)
    return rmsnorm(o, g) * (1.0 - lam_init)


def diff_attn_prompt(q, k, v, lam, g, lam_init):
    B, S = q.shape[:2]
    nb = S // Q_BLOCK
    qb = q.reshape(B, nb, Q_BLOCK, H_D, 2, DH_D).transpose(1, 0, 2, 3, 4, 5)
    key_chunk = jnp.arange(S) // CHUNK

    def one(args):
        qblk, i = args
        q_chunk = (i * Q_BLOCK + jnp.arange(Q_BLOCK)) // CHUNK
        mask = key_chunk[None, :] <= q_chunk[:, None]
        return diff_attend(qblk, k, v, mask, lam, g, lam_init)

    o = lax.map(one, (qb, jnp.arange(nb)))
    return o.transpose(1, 0, 2, 3, 4).reshape(B, S, H_D, DV_D)


def merge(o_gla, gg, gla_g, o_diff, w_out):
    B, H, T, _ = o_gla.shape
    g_out = rmsnorm(o_gla.transpose(0, 2, 1, 3), gla_g).reshape(B, T, H_G * DV_G) * jax.nn.silu(gg)
    d_out = o_diff.reshape(B, T, H_D * DV_D)
    return jnp.concatenate([g_out, d_out], axis=-1) @ w_out


def setup_inputs(seed: int = 0) -> dict:
    key = jax.random.key(seed)
    ks = iter(jax.random.split(key, 40))
    nrm = lambda shape, scale: jax.random.normal(next(ks), shape, jnp.float32) * scale
    gain = lambda n: 1.0 + nrm((DEPTH, n), 0.05)
    return {
        "x_prompt": nrm((BATCH, SEQ, D_MODEL), 1.0),
        "x_sample": nrm((DEC_BATCH, DEC_SEQ, D_MODEL), 1.0),
        "state_gla": nrm((DEPTH, DEC_BATCH, H_G, DK_G, DV_G), 0.1),
        "cache_diff_k": nrm((DEPTH, DEC_BATCH, PAST_LEN, H_D, 2 * DH_D), 1.0),
        "cache_diff_v": nrm((DEPTH, DEC_BATCH, PAST_LEN, H_D, DV_D), 1.0),
        "w_in": nrm((DEPTH, D_MODEL, D_IN), D_MODEL ** -0.5),
        "w_gate_a2": nrm((DEPTH, GATE_RANK, H_G * DK_G), GATE_RANK ** -0.5),
        "b_gate_a": nrm((DEPTH, H_G * DK_G), 0.1),
        "gla_norm_g": gain(DV_G),
        "lambda_q1": nrm((DEPTH, DH_D), 0.1),
        "lambda_k1": nrm((DEPTH, DH_D), 0.1),
        "lambda_q2": nrm((DEPTH, DH_D), 0.1),
        "lambda_k2": nrm((DEPTH, DH_D), 0.1),
        "diff_norm_g": gain(DV_D),
        "w_out": nrm((DEPTH, MIX_WIDTH, D_MODEL), MIX_WIDTH ** -0.5),
        "mix_pre_g": gain(D_MODEL),
        "mix_post_g": gain(D_MODEL),
        "ffn1_pre_g": gain(D_MODEL),
        "ffn1_post_g": gain(D_MODEL),
        "ffn1_w_gate": nrm((DEPTH, D_MODEL, D_FF), D_MODEL ** -0.5),
        "ffn1_w_up": nrm((DEPTH, D_MODEL, D_FF), D_MODEL ** -0.5),
        "ffn1_w_down": nrm((DEPTH, D_FF, D_MODEL), D_FF ** -0.5),
        "ffn2_pre_g": gain(D_MODEL),
        "ffn2_post_g": gain(D_MODEL),
        "ffn2_w_gate": nrm((DEPTH, D_MODEL, D_FF), D_MODEL ** -0.5),
        "ffn2_w_up": nrm((DEPTH, D_MODEL, D_FF), D_MODEL ** -0.5),
        "ffn2_w_down": nrm((DEPTH, D_FF, D_MODEL), D_FF ** -0.5),
    }


def reference(x_prompt, x_sample, state_gla, cache_diff_k, cache_diff_v,
              w_in, w_gate_a2, b_gate_a, gla_norm_g,
              lambda_q1, lambda_k1, lambda_q2, lambda_k2, diff_norm_g, w_out,
              mix_pre_g, mix_post_g,
              ffn1_pre_g, ffn1_post_g, ffn1_w_gate, ffn1_w_up, ffn1_w_down,
              ffn2_pre_g, ffn2_post_g, ffn2_w_gate, ffn2_w_up, ffn2_w_down):
    xp, xs = x_prompt, x_sample
    Bs, T = xs.shape[:2]
    P = cache_diff_k.shape[2]
    sg_p, k_p, v_p, sg_s, k_s, v_s = [], [], [], [], [], []
    for l in range(DEPTH):
        lam_init = 0.8 - 0.6 * math.exp(-0.3 * l)
        lam = (jnp.exp(jnp.sum(lambda_q1[l] * lambda_k1[l]).astype(jnp.float32))
               - jnp.exp(jnp.sum(lambda_q2[l] * lambda_k2[l]).astype(jnp.float32)) + lam_init)
        ffn1 = (ffn1_pre_g[l], ffn1_post_g[l], ffn1_w_gate[l], ffn1_w_up[l], ffn1_w_down[l])
        ffn2 = (ffn2_pre_g[l], ffn2_post_g[l], ffn2_w_gate[l], ffn2_w_up[l], ffn2_w_down[l])

        xp = ffn_half(xp, *ffn1)
        gq, gk, gv, la, gg, dq, dk, dv = project(rmsnorm(xp, mix_pre_g[l]), w_in[l], w_gate_a2[l], b_gate_a[l])
        o_gla, S_p = gla_prompt(gq, gk, gv, la)
        o_diff = diff_attn_prompt(dq, dk, dv, lam, diff_norm_g[l], lam_init)
        xp = xp + rmsnorm(merge(o_gla, gg, gla_norm_g[l], o_diff, w_out[l]), mix_post_g[l])
        xp = ffn_half(xp, *ffn2)
        sg_p.append(S_p.astype(x_prompt.dtype))
        k_p.append(dk.reshape(dk.shape[0], dk.shape[1], H_D, 2 * DH_D))
        v_p.append(dv)

        xs = ffn_half(xs, *ffn1)
        gq, gk, gv, la, gg, dq, dk, dv = project(rmsnorm(xs, mix_pre_g[l]), w_in[l], w_gate_a2[l], b_gate_a[l])
        S_s, o_gla = gla_block(state_gla[l], gq, gk, gv, la)
        keys = jnp.concatenate([cache_diff_k[l].reshape(Bs, P, H_D, 2, DH_D), dk], axis=1)
        vals = jnp.concatenate([cache_diff_v[l], dv], axis=1)
        o_diff = diff_attend(dq, keys, vals, None, lam, diff_norm_g[l], lam_init)
        xs = xs + rmsnorm(merge(o_gla, gg, gla_norm_g[l], o_diff, w_out[l]), mix_post_g[l])
        xs = ffn_half(xs, *ffn2)
        sg_s.append(S_s.astype(state_gla.dtype))
        k_s.append(dk.reshape(Bs, T, H_D, 2 * DH_D))
        v_s.append(dv)

    return (xp, xs, jnp.stack(sg_p), jnp.stack(k_p), jnp.stack(v_p),
            jnp.stack(sg_s), jnp.stack(k_s), jnp.stack(v_s))
```

```python
import concourse.bass as bass
import concourse.mybir as mybir

F32 = mybir.dt.float32
BF16 = mybir.dt.bfloat16
ALU = mybir.AluOpType
AF = mybir.ActivationFunctionType
PG = 64

_DSZ = {F32: 4, BF16: 2, mybir.dt.int32: 4, mybir.dt.uint32: 4, mybir.dt.float16: 2,
        mybir.dt.uint16: 2, mybir.dt.int16: 2, mybir.dt.uint8: 1, mybir.dt.int8: 1}


class Op:
    __slots__ = ("eng", "fn", "deps", "idx", "tick", "sem", "semval", "is_dma", "need_inc", "name")

    def __init__(self, eng, fn, is_dma=False, name=""):
        self.eng = eng
        self.fn = fn
        self.deps = []
        self.is_dma = is_dma
        self.need_inc = False
        self.tick = None
        self.sem = None
        self.semval = None
        self.name = name


class Res:
    __slots__ = ("w", "r")

    def __init__(self):
        self.w = None
        self.r = []


class KB:
    ENGS = ("pe", "act", "dve", "pool", "sp")

    def __init__(self, nc, n_dma_sems=(24, 40, 2)):
        self.nc = nc
        self.ops = {e: [] for e in self.ENGS}
        self.res = {}
        self.tinfo = {}
        self.sb_off = 16640
        self.ps_off = 0
        self.n_dma_sems = dict(sp=n_dma_sems[0], pool=n_dma_sems[1], act=n_dma_sems[2])
        self.dma_hist = {"sp": [], "pool": [], "act": []}
        self.out_dmas = []
        self.nops = 0

    def sb(self, name, shape, dtype, at=None):
        nbytes = _DSZ[dtype]
        for s in shape[1:]:
            nbytes *= s
        if at is None:
            at = (self.sb_off + 63) // 64 * 64
            self.sb_off = at + nbytes
        t = self.nc.alloc_sbuf_tensor_at(name, list(shape), dtype, offset=at)
        self.tinfo[t.name] = ("S", at, nbytes)
        return t

    def sb_mark(self):
        return self.sb_off

    def sb_reset(self, mark):
        self.sb_off = mark

    def ps(self, name, shape, dtype=F32):
        nbytes = _DSZ[dtype]
        for s in shape[1:]:
            nbytes *= s
        t = self.nc.alloc_psum_tensor(name, list(shape), dtype)
        nb = (nbytes + 2047) // 2048
        self.tinfo[t.name] = ("P", self.ps_off, nb * 2048)
        self.ps_off += nb * 2048
        return t

    def dram_token(self, name):
        return ("D", name)

    def pages(self, ap):
        if isinstance(ap, tuple):
            return [ap]
        name = ap.tensor.name
        info = self.tinfo.get(name)
        if info is None:
            return []
        space, base, bpp = info
        dsz = _DSZ[ap.dtype]
        pstep = bpp // dsz
        apl = ap.ap
        fo = ap.offset % pstep
        ext = 0
        for (s, c) in apl[1:]:
            if s > 0:
                ext += (c - 1) * s
        lo = base + fo * dsz
        hi = base + (fo + ext + 1) * dsz - 1
        return [(space, p) for p in range(lo // PG, hi // PG + 1)]

    def _add(self, op, reads, writes):
        rp = set()
        for a in reads:
            rp.update(self.pages(a))
        wp = set()
        for a in writes:
            wp.update(self.pages(a))
        for p in list(rp) + list(wp):
            if p[0] == "P":
                wp.add(("PB", p[1] // (2048 // PG)))
        deps = set()
        for p in rp:
            r = self.res.get(p)
            if r is None:
                r = self.res[p] = Res()
            w = r.w
            if w is not None:
                if w.eng == op.eng and not w.is_dma and not op.is_dma and op.eng == "pe":
                    pass
                else:
                    deps.add(w)
        for p in wp:
            r = self.res.get(p)
            if r is None:
                r = self.res[p] = Res()
            w = r.w
            if w is not None:
                if w.eng == op.eng and not w.is_dma and not op.is_dma and op.eng == "pe":
                    pass
                else:
                    deps.add(w)
            for rd in r.r:
                if rd is op:
                    continue
                if rd.eng == op.eng and not rd.is_dma and not op.is_dma and op.eng == "pe":
                    continue
                deps.add(rd)
        for p in rp:
            r = self.res[p]
            if op.is_dma:
                r.r.append(op)
            else:
                r.r = [x for x in r.r if x.is_dma or x.eng != op.eng]
                r.r.append(op)
        for p in wp:
            r = self.res[p]
            r.w = op
            r.r = []
        deps.discard(op)
        op.deps = list(deps)
        op.idx = self.nops
        self.nops += 1
        self.ops[op.eng].append(op)
        return op

    def op(self, eng, fn, reads=(), writes=(), name=""):
        return self._add(Op(eng, fn, False, name), reads, writes)

    def dma(self, q, out, in_, extra_reads=(), extra_writes=(), is_output=False, name=""):
        o = Op(q, lambda e, out=out, in_=in_: e.dma_start(out=out, in_=in_), True, name)
        hist = self.dma_hist[q]
        n = self.n_dma_sems[q]
        k = len(hist)
        o.sem = (q, k % n)
        o.semval = 16 * (k // n + 1)
        self._add(o, [in_] + list(extra_reads), [out] + list(extra_writes))
        if k >= n:
            prev = hist[k - n]
            if prev not in o.deps:
                o.deps.append(prev)
        hist.append(o)
        if is_output:
            self.out_dmas.append(o)
        return o

    def mm(self, out, lhsT, rhs, start=True, stop=True, **kw):
        return self.op("pe", lambda e: e.matmul(out, lhsT, rhs, start=start, stop=stop, **kw),
                       [lhsT, rhs] + ([] if start else []), [out])

    def transpose(self, out, in_, ident):
        return self.op("pe", lambda e: e.transpose(out, in_, ident), [in_, ident], [out])

    def act(self, out, in_, func, bias=None, scale=1.0, accum_out=None, eng="act"):
        reads = [in_]
        kw = {}
        if bias is not None:
            kw["bias"] = bias
            if not isinstance(bias, (int, float)):
                reads.append(bias)
        if not isinstance(scale, (int, float)):
            reads.append(scale)
        writes = [out]
        if accum_out is not None:
            kw["accum_out"] = accum_out
            writes.append(accum_out)
        return self.op("act", lambda e: e.activation(out, in_, func, scale=scale, **kw), reads, writes)

    def tt(self, eng, out, in0, in1, op):
        return self.op(eng, lambda e: e.tensor_tensor(out, in0, in1, op), [in0, in1], [out])

    def ts(self, eng, out, in0, s1, s2, op0, op1=None):
        reads = [in0]
        if not isinstance(s1, (int, float)):
            reads.append(s1)
        if s2 is not None and not isinstance(s2, (int, float)):
            reads.append(s2)
        if op1 is None:
            s2 = 0.0
            op1 = ALU.add
        return self.op(eng, lambda e: e.tensor_scalar(out, in0, s1, s2, op0, op1), reads, [out])

    def stt(self, eng, out, in0, scalar, in1, op0, op1):
        reads = [in0, in1]
        if not isinstance(scalar, (int, float)):
            reads.append(scalar)
        return self.op(eng, lambda e: e.scalar_tensor_tensor(out, in0, scalar, in1, op0, op1), reads, [out])

    def copy(self, eng, out, in_):
        if eng == "act":
            return self.op("act", lambda e: e.copy(out, in_), [in_], [out])
        return self.op(eng, lambda e: e.tensor_copy(out, in_), [in_], [out])

    def memset(self, eng, out, val):
        return self.op(eng, lambda e: e.memset(out, val), [], [out])

    def recip(self, out, in_):
        return self.op("dve", lambda e: e.reciprocal(out, in_), [in_], [out])

    def scan(self, out, d0, d1, init, op0, op1, eng="dve"):
        reads = [d0, d1]
        if not isinstance(init, (int, float)):
            reads.append(init)
        return self.op(eng, lambda e: e.tensor_tensor_scan(out, d0, d1, init, op0, op1), reads, [out])

    def emit(self):
        nc = self.nc
        for e in self.ENGS:
            for o in self.ops[e]:
                for d in o.deps:
                    d.need_inc = True
        for o in self.out_dmas:
            o.need_inc = True
        for q in self.dma_hist:
            for o in self.dma_hist[q]:
                o.need_inc = True
        for e in ("pe", "act", "dve", "pool"):
            t = 0
            for o in self.ops[e]:
                if o.is_dma:
                    continue
                if o.need_inc:
                    t += 1
                    o.tick = t
        sems = {}
        import contextlib
        with contextlib.ExitStack() as st:
            for e in ("pe", "act", "dve", "pool"):
                sems[e] = st.enter_context(nc.semaphore("c_" + e))
            for q in ("sp", "pool", "act"):
                for i in range(self.n_dma_sems[q]):
                    sems[(q, i)] = st.enter_context(nc.semaphore(f"d_{q}{i}"))
            block = st.enter_context(nc.Block())

            def gen(ename):
                def body(eng):
                    waited = {}
                    for o in self.ops[ename]:
                        need = {}
                        for d in o.deps:
                            if d.is_dma:
                                key, val = d.sem, d.semval
                            else:
                                key, val = d.eng, d.tick
                            if waited.get(key, 0) >= val:
                                continue
                            if need.get(key, 0) < val:
                                need[key] = val
                        for key, val in need.items():
                            eng.wait_ge(sems[key], val)
                            waited[key] = val
                        ins = o.fn(eng)
                        if o.is_dma:
                            ins.then_inc(sems[o.sem], 16)
                        elif o.need_inc:
                            ins.then_inc(sems[o.eng], 1)
                    if ename == "sp":
                        need = {}
                        alld = self.out_dmas + [d for q in self.dma_hist for d in self.dma_hist[q]]
                        for d in alld:
                            if need.get(d.sem, 0) < d.semval:
                                need[d.sem] = d.semval
                        for key, val in need.items():
                            if waited.get(key, 0) < val:
                                eng.wait_ge(sems[key], val)
                return body

            block.tensor(gen("pe"))
            block.scalar(gen("act"))
            block.vector(gen("dve"))
            block.gpsimd(gen("pool"))
            block.sync(gen("sp"))

import numpy as np
from concourse.bass_utils import run_bass_kernel_spmd

D = 1024
DFF = 2816
NJ = 22
T_FULL = 4096
NTK = 512
NS = 4
TS = 16
TSP = 32
PAST = 2048
EPS = 1e-6
LAM_INIT = 0.2
C_GQ, C_GK, C_GV, C_GR, C_GG, C_DQ, C_DK, C_DV = 0, 256, 512, 1024, 1040, 1552, 2064, 2576
K_F1PRE, K_F1POST, K_MPRE, K_MPOST, K_F2PRE, K_F2POST, K_BA, K_GLAG, K_DIFFG, K_LAM = 0, 8, 16, 24, 32, 40, 48, 50, 51, 52
NCST = 56


class Prog:
    def __init__(self, nc, n_tiles=8, do_sample=True):
        self.nc = nc
        self.kb = KB(nc)
        self.dbg_names = set()
        self.n_tiles = n_tiles
        self.do_sample = do_sample
        self.T = n_tiles * NTK
        self.build()

    def dram_in(self, name, shape):
        return self.nc.dram_tensor(name, list(shape), F32, kind="ExternalInput").ap()

    def dram_out(self, name, shape):
        return self.nc.dram_tensor(name, list(shape), F32, kind="ExternalOutput").ap()

    def build(self):
        kb = self.kb
        T = self.T
        TT = T + (NS * TSP if self.do_sample else 0)
        self.xT = self.dram_in("xT", [D, TT])
        self.cst_d = self.dram_in("cst", [128, NCST])
        self.w_in = self.dram_in("w_in", [D, 3088])
        self.w_a2 = self.dram_in("w_a2", [16, 256])
        self.w_gr = self.dram_in("w_gr", [128, 256])
        self.w_out = self.dram_in("w_out", [D, D])
        self.fw = {}
        for f in (1, 2):
            self.fw[f] = (self.dram_in(f"f{f}g", [D, DFF]), self.dram_in(f"f{f}u", [D, DFF]),
                          self.dram_in(f"f{f}d", [DFF, D]))
        if self.do_sample:
            self.state_s = self.dram_in("state_s", [NS, 128, 2, 128])
            self.kTc = self.dram_in("kTc", [NS, 4, 128, PAST])
            self.vc = self.dram_in("vc", [NS, PAST, 4, 128])
        self.ident_d = self.dram_in("ident", [128, 128])
        self.cmask_d = self.dram_in("cmask", [128, 128])
        self.yT = self.dram_out("yT", [D, TT])
        self.kT_out = self.dram_out("kT_out", [512, TT])
        self.v_out = self.dram_out("v_out", [TT, 512])
        self.S_p = self.dram_out("S_p", [128, 2, 128])
        if self.do_sample:
            self.S_s = self.dram_out("S_s", [NS, 128, 2, 128])

        sb = kb.sb
        self.cst = sb("cst_sb", [128, NCST], F32)
        self.dcst = sb("dcst", [128, 32], F32)
        self.ones_bf = sb("ones_bf", [128, 128], BF16)
        self.ident = sb("ident_sb", [128, 128], BF16)
        self.cmask = sb("cmask_sb", [128, 128], BF16)
        self.smask_p = sb("smask_p", [128, NTK], F32)
        self.smask_s = sb("smask_s", [128, NS * TSP], F32)
        self.negv_s = sb("negv_s", [128, NS * TSP], F32)
        self.wa2 = sb("wa2_sb", [16, 256], BF16)
        self.wgr_f = sb("wgr_f", [128, 256], F32)
        self.ones_f = self.wgr_f[:, 0:128]
        self.wgr = sb("wgr", [128, 8, 32], BF16)
        self.KT = sb("KT", [128, 4, T_FULL], BF16)
        self.VS = sb("VS", [128, 32, 512], BF16)
        self.x = sb("x", [128, 8, NTK], F32)
        self.hb = sb("hb", [128, 8, NTK], BF16)
        self.f = sb("f", [128, 8, NTK], F32)
        self.rstd = sb("rstd", [128, NTK], F32)
        self.NSLOT = 5
        self.wring = [sb(f"wr{i}", [128, 4096], BF16) for i in range(self.NSLOT)]
        um = kb.sb_mark()
        self.a = sb("a", [128, NJ, NTK], BF16)
        uend = kb.sb_mark()
        kb.sb_reset(um)
        self.gg_bf = sb("gg_bf", [128, 4, NTK], BF16)
        self.dq1 = sb("dq1", [128, 4, NTK], BF16)
        self.dq2 = sb("dq2", [128, 4, NTK], BF16)
        self.gv_tok = sb("gv_tok", [128, 4, 512], BF16)
        self.qt = sb("qt", [128, 4, NTK], BF16)
        self.kt = sb("kt", [128, 2, NTK], BF16)
        self.o_sb = sb("o_sb", [128, 4, NTK], F32)
        self.merge = sb("merge", [128, 8, NTK], BF16)
        kb.sb_reset(max(uend, kb.sb_mark()))
        self.x2 = sb("x2", [128, 8, NTK], F32, at=kb.tinfo[self.o_sb.name][1])
        assert kb.tinfo[self.merge.name][1] == kb.tinfo[self.o_sb.name][1] + 8192
        self.tfp = [sb(f"tf{i}", [128, NTK], F32) for i in range(4)]
        self.tbp = [sb(f"tb{i}", [128, NTK], BF16) for i in range(4)]
        self.tfi = 0
        self.tbi = 0
        self.S = sb("S", [128, 2, 128], F32)
        self.S_bf = sb("S_bf", [128, 2, 128], BF16)
        self.ktok = sb("ktok", [128, 2, 128], BF16)
        self.at_sb = sb("at_sb", [128, 128], BF16)
        self.gr_bf = sb("gr_bf", [32, NTK], BF16)
        self.eL = [sb(f"eL{i}", [128, 8], F32) for i in range(2)]
        self.cache_i = 0
        self.vtok_f = [sb(f"vtokf{i}", [128, 512], F32) for i in range(1)]
        self.vtf_i = 0
        print('SBUF used', kb.sb_off)
        assert kb.sb_off <= 229312, kb.sb_off
        ktb = kb.tinfo[self.KT.name][1]
        vsb = kb.tinfo[self.VS.name][1]
        self.NCACHE = 6
        self.kc_sb = [sb(f"kc{i}", [128, PAST], BF16, at=ktb + i * 4096) for i in range(self.NCACHE)]
        self.vc_sb = [sb(f"vcs{i}", [128, 16, 128], BF16, at=vsb + i * 4096) for i in range(self.NCACHE)]
        self.KT_s = sb("KT_s", [128, 4, NS * TSP], BF16, at=ktb + 24576)
        self.dvs_tok = sb("dvs_tok", [TSP, NS, 512], BF16, at=ktb + 24576 + 1024)
        self.gvs_tok = sb("gvs_tok", [TSP, NS, 512], BF16, at=vsb + 24576)
        self.cache_loaded = 0
        self.pb = [kb.ps(f"pb{i}", [128, 512], F32) for i in range(8)]
        self.pa_i = 0
        self.pb_i = 0

        self.setup_consts()
        self.blk_ids = {}
        for i, sp_ in enumerate(self.tile_specs()):
            self.blk_ids[sp_] = i
        self.wbf = self.nc.dram_tensor("wbf16", [len(self.blk_ids), 128, 4096], BF16).ap()
        self.converted = set()
        self.conv_next = 0
        self.wq = []
        self.wq_pos = 0
        self.wq_issued = 0
        specs = []
        for t in range(self.n_tiles):
            specs += self.tile_specs()
        if self.do_sample:
            specs += self.tile_specs()
        import os
        st = int(os.environ.get("KSTAGE", "9"))
        if st < 9:
            one = self.tile_specs()
            ntile = self.n_tiles + (1 if self.do_sample else 0)
            if st == 1:
                one = one[:20]
            else:
                one = one[:28]
            specs = one * ntile
        self.wq = specs
        self.load_x(0, NTK)
        self.w_convert_ahead(8)
        for _ in range(self.NSLOT - 2):
            self.w_issue()
        self.ffn_prenorm(1, NTK, self.x2)
        tiles = [(t * NTK, NTK, True) for t in range(self.n_tiles)]
        if self.do_sample:
            tiles.append((T, NS * TSP, False))
        for i, (t0, n_, pr) in enumerate(tiles):
            nxt = (tiles[i + 1][0], tiles[i + 1][1]) if i + 1 < len(tiles) else None
            self.tile(t0, n_, prompt=pr, first=(i == 0), last=(pr and i == self.n_tiles - 1), nxt=nxt)
        kb.emit()

    def dbg(self, name, ap):
        import os
        if os.environ.get("KDBG", "0") != "1" or name in self.dbg_names:
            return
        self.dbg_names.add(name)
        d = self.nc.dram_tensor("dbg_" + name, list(ap.shape), F32, kind="ExternalOutput").ap()
        self.kb.dma("pool", d, ap, is_output=True)

    def tf(self):
        t = self.tfp[self.tfi % len(self.tfp)]
        self.tfi += 1
        return t

    def tb(self):
        t = self.tbp[self.tbi % len(self.tbp)]
        self.tbi += 1
        return t

    def psA(self):
        t = self.pb[self.pa_i % 4]
        self.pa_i += 1
        return t

    def psB(self):
        t = self.pb[4 + self.pb_i % 4]
        self.pb_i += 1
        return t

    def setup_consts(self):
        kb = self.kb
        kb.dma("sp", self.cst[:], self.cst_d)
        kb.dma("pool", self.ident[:], self.ident_d)
        kb.dma("pool", self.cmask[:], self.cmask_d)
        kb.dma("pool", self.wa2[:], self.w_a2)
        kb.memset("dve", self.ones_bf[:], 1.0)
        kb.memset("dve", self.ones_f, 1.0)
        kb.memset("dve", self.smask_p[:], 1.0)
        kb.memset("dve", self.smask_p[:].rearrange("p (c t) -> p c t", t=128)[:, :, 0:1], 0.0)
        kb.memset("dve", self.smask_s[:], 1.0)
        kb.memset("dve", self.smask_s[:].rearrange("p (c t) -> p c t", t=TSP)[:, :, 0:1], 0.0)
        kb.memset("dve", self.negv_s[:], 0.0)
        kb.memset("dve", self.negv_s[:].rearrange("p (c t) -> p c t", t=TSP)[:, :, 0:TS], -1.0 / 16.0)
        kb.memset("dve", self.S[:], 0.0)
        kb.memset("dve", self.S_bf[:], 0.0)
        dc = self.dcst
        c = self.cst
        kb.ts("dve", dc[:, 0:8], c[:, K_F1POST:K_F1POST + 8], 0.5, None, ALU.mult)
        kb.ts("dve", dc[:, 8:16], c[:, K_F2POST:K_F2POST + 8], 0.5, None, ALU.mult)
        kb.ts("dve", dc[:, 16:18], c[:, K_BA:K_BA + 2], -1.0, None, ALU.mult)
        kb.ts("dve", dc[:, 18:19], c[:, K_DIFFG:K_DIFFG + 1], 1.0 - LAM_INIT, None, ALU.mult)
        kb.tt("dve", dc[:, 20:21], c[:, K_LAM:K_LAM + 1], c[:, K_LAM + 1:K_LAM + 2], ALU.mult)
        kb.tt("dve", dc[:, 21:22], c[:, K_LAM + 2:K_LAM + 3], c[:, K_LAM + 3:K_LAM + 4], ALU.mult)
        p = self.psB()
        kb.mm(p[:, 0:2], self.ones_f, dc[:, 20:22])
        kb.act(dc[:, 22:24], p[:, 0:2], AF.Exp)
        kb.tt("dve", dc[:, 24:25], dc[:, 23:24], dc[:, 22:23], ALU.subtract)
        kb.ts("dve", dc[:, 19:20], dc[:, 24:25], -LAM_INIT, None, ALU.add)
        self.neg_lam = dc[:, 19:20]
        kb.dma("sp", self.wgr_f[:], self.w_gr)
        kb.copy("dve", self.wgr[:].rearrange("p c f -> p (c f)"), self.wgr_f[:])
        self.dg08 = dc[:, 18:19]

    def tile_specs(self):
        s = []
        for f in (1, 2):
            ff = []
            for jb in range(6):
                j0 = jb * 4
                j1 = min(NJ, j0 + 4)
                ff.append(("g", f, j0, j1))
                ff.append(("u", f, j0, j1))
            for mb in range(4):
                for jh in range(2):
                    ff.append(("d", f, mb, jh))
            if f == 1:
                s += ff
                s += [("in", C_GQ, 512), ("in", C_GV, 512), ("in", C_GG, 512), ("in", C_DQ, 512),
                      ("in", C_DK, 512), ("in", C_DV, 512), ("out", 0), ("out", 1)]
            else:
                s += ff
        return s

    def w_src_dst(self, spec, slot):
        wr = self.wring[slot]
        kind = spec[0]
        if kind in ("g", "u"):
            _, f, j0, j1 = spec
            w = self.fw[f][0 if kind == "g" else 1]
            n = (j1 - j0) * 128
            src = w.rearrange("(c p) f -> p c f", p=128)[:, :, j0 * 128:j1 * 128]
            dst = wr[:, 0:8 * n].rearrange("p (c f) -> p c f", c=8)
        elif kind == "d":
            _, f, mb, jh = spec
            w = self.fw[f][2]
            src = w.rearrange("(j p) m -> p j m", p=128)[:, jh * 11:(jh + 1) * 11, mb * 256:(mb + 1) * 256]
            dst = wr[:, 0:11 * 256].rearrange("p (j m) -> p j m", j=11)
        elif kind == "in":
            _, c0, n = spec
            src = self.w_in.rearrange("(c p) f -> p c f", p=128)[:, :, c0:c0 + n]
            dst = wr[:, 0:8 * n].rearrange("p (c f) -> p c f", c=8)
        elif kind == "out":
            _, mh = spec
            src = self.w_out.rearrange("(c p) f -> p c f", p=128)[:, :, mh * 512:(mh + 1) * 512]
            dst = wr[:, 0:8 * 512].rearrange("p (c f) -> p c f", c=8)
        return src, dst

    def w_issue(self):
        if self.wq_issued >= len(self.wq):
            return
        i = self.wq_issued
        spec = self.wq[i]
        src, dst = self.w_src_dst(spec, i % self.NSLOT)
        b = self.blk_ids[spec]
        n = 1
        for d_ in dst.shape[1:]:
            n *= d_
        sc3 = self.wbf[b][:, 0:n].rearrange("p (c f) -> p c f", c=dst.shape[1])
        tok = ("D", "wbf%d" % b)
        wr = self.wring[i % self.NSLOT]
        if b not in self.converted:
            self.converted.add(b)
            self.kb.dma("pool", dst, src)
            self.kb.dma("sp", self.wbf[b][:, 0:n], wr[:, 0:n], extra_writes=[tok])
        else:
            self.kb.dma("sp", wr[:, 0:n], self.wbf[b][:, 0:n], extra_reads=[tok])
        self.wq_issued += 1

    def w_convert_ahead(self, upto):
        return
        nb = len(self.blk_ids)
        while self.conv_next < min(upto, nb):
            spec = self.wq[self.conv_next]
            b = self.blk_ids[spec]
            src, dst = self.w_src_dst(spec, 0)
            n = 1
            for d_ in dst.shape[1:]:
                n *= d_
            sc3 = self.wbf[b][:, 0:n].rearrange("p (c f) -> p c f", c=dst.shape[1])
            self.converted.add(b)
            self.kb.dma("pool", sc3, src, extra_writes=[("D", "wbf%d" % b)])
            self.conv_next += 1

    def w_get(self, spec):
        i = self.wq_pos
        assert self.wq[i] == spec, (self.wq[i], spec)
        self.w_convert_ahead(i + 12)
        self.w_issue()
        _, dst = self.w_src_dst(spec, i % self.NSLOT)
        self.wq_pos += 1
        return dst

    def rms_rstd(self, chunks, N, dim, use_psA=False, presq=False):
        kb = self.kb
        n = len(chunks)
        if not presq:
            for c, ch in enumerate(chunks):
                kb.act(self.hb[:, c, 0:N], ch, AF.Square)
        p = self.psA() if use_psA else self.psB()
        for c in range(n):
            kb.mm(p[:, 0:N], self.ones_bf[:, :], self.hb[:, c, 0:N], start=(c == 0), stop=(c == n - 1))
        kb.act(self.rstd[:, 0:N], p[:, 0:N], AF.Ln, bias=EPS, scale=1.0 / dim)
        kb.act(self.rstd[:, 0:N], self.rstd[:, 0:N], AF.Exp, scale=-0.5)
        return self.rstd[:, 0:N]

    def norm_to_hb(self, src, N, gcol):
        kb = self.kb
        r = self.rms_rstd([src[:, c, 0:N] for c in range(8)], N, D)
        for c in range(8):
            g = self.cst[:, gcol + c:gcol + c + 1]
            if False:
                t = self.tf()
                kb.ts("pool", t[:, 0:N], src[:, c, 0:N], g, None, ALU.mult)
                kb.tt("pool", self.hb[:, c, 0:N], t[:, 0:N], r, ALU.mult)
            else:
                kb.stt("dve", self.hb[:, c, 0:N], src[:, c, 0:N], g, r, ALU.mult, ALU.mult)

    def residual_update(self, N, gtile, gcol, final_tok0=None, xsrc=None, mid_hook=None):
        kb = self.kb
        r = self.rms_rstd([self.f[:, c, 0:N] for c in range(8)], N, D, presq=True)
        if mid_hook is not None:
            mid_hook()
        if xsrc is None:
            xsrc = self.x
        for c in range(8):
            fc = self.f[:, c, 0:N]
            g = gtile[:, gcol + c:gcol + c + 1]
            dst = self.x[:, c, 0:N] if final_tok0 is None else fc
            if False:
                kb.ts("pool", fc, fc, g, None, ALU.mult)
                kb.tt("pool", fc, fc, r, ALU.mult)
                kb.tt("pool", dst, self.x[:, c, 0:N], fc, ALU.add)
            else:
                kb.stt("dve", fc, fc, g, r, ALU.mult, ALU.mult)
                kb.tt("dve", dst, xsrc[:, c, 0:N], fc, ALU.add)
            if final_tok0 is not None:
                kb.dma("sp", self.yT[c * 128:(c + 1) * 128, final_tok0:final_tok0 + N], fc, is_output=True)

    def ffn_prenorm(self, f, N, src):
        kb = self.kb
        pre = K_F1PRE if f == 1 else K_F2PRE
        SQ0 = NJ - 8
        for c in range(8):
            kb.act(self.hb[:, c, 0:N], src[:, c, 0:N], AF.Copy, scale=self.cst[:, pre + c:pre + c + 1])
            kb.act(self.a[:, SQ0 + c, 0:N], src[:, c, 0:N], AF.Square)
        p = self.psB()
        for c in range(8):
            kb.mm(p[:, 0:N], self.ones_bf[:, :], self.a[:, SQ0 + c, 0:N], start=(c == 0), stop=(c == 7))
        r2 = self.vtok_f[0]
        kb.act(r2[:, 0:N], p[:, 0:N], AF.Ln, bias=EPS, scale=1.0 / D)
        kb.act(r2[:, 0:N], r2[:, 0:N], AF.Exp, scale=-0.5)

    def ffn(self, f, N, xsrc=None, mid_hook=None):
        kb = self.kb
        r = self.vtok_f[0][:, 0:N]
        for jb in range(6):
            j0 = jb * 4
            j1 = min(NJ, j0 + 4)
            wg = self.w_get(("g", f, j0, j1))
            wu = self.w_get(("u", f, j0, j1))
            for j in range(j0, j1):
                jl = j - j0
                pg = self.psA()
                pu = self.psA()
                for k in range(8):
                    kb.mm(pg[:, 0:N], wg[:, k, jl * 128:(jl + 1) * 128], self.hb[:, k, 0:N], start=(k == 0), stop=(k == 7))
                for k in range(8):
                    kb.mm(pu[:, 0:N], wu[:, k, jl * 128:(jl + 1) * 128], self.hb[:, k, 0:N], start=(k == 0), stop=(k == 7))
                t1 = self.tf()
                kb.tt("dve", t1[:, 0:N], pg[:, 0:N], r, ALU.mult)
                sg = self.tb()
                kb.act(sg[:, 0:N], t1[:, 0:N], AF.Silu)
                t2 = self.tf()
                kb.tt("dve", t2[:, 0:N], pu[:, 0:N], r, ALU.mult)
                kb.tt("dve", self.a[:, j, 0:N], t2[:, 0:N], sg[:, 0:N], ALU.mult)
        for mb in range(4):
            p0 = self.psB()
            p1 = self.psB()
            pp = (p0, p1)
            for jh in range(2):
                wd = self.w_get(("d", f, mb, jh))
                for ml in range(2):
                    for jl in range(11):
                        j = jh * 11 + jl
                        kb.mm(pp[ml][:, 0:N], wd[:, jl, ml * 128:(ml + 1) * 128], self.a[:, j, 0:N],
                              start=(j == 0), stop=(j == NJ - 1))
            for ml in range(2):
                kb.act(self.hb[:, mb * 2 + ml, 0:N], pp[ml][:, 0:N], AF.Square)
                kb.copy("act", self.f[:, mb * 2 + ml, 0:N], pp[ml][:, 0:N])
        self.dbg(f"a{f}", self.a[:, :, 0:N])
        self.dbg(f"f{f}", self.f[:, :, 0:N])
        self.residual_update(N, self.dcst, 0 if f == 1 else 8, final_tok0=(self.cur_tok0 if f == 2 else None),
                             xsrc=xsrc, mid_hook=mid_hook)
        self.dbg(f"rstdf{f}", self.rstd[:, 0:N])

    def proj_fm(self, wblk, col0, N, use_psB=False):
        kb = self.kb
        p = self.psB() if use_psB else self.psA()
        for k in range(8):
            kb.mm(p[:, 0:N], wblk[:, k, col0:col0 + 128], self.hb[:, k, 0:N], start=(k == 0), stop=(k == 7))
        return p

    def mixer(self, tok0, N, prompt):
        kb = self.kb
        L = 128 if prompt else TSP
        nchunk = N // L
        self.norm_to_hb(self.x, N, K_MPRE)
        wgr = self.wgr
        p = self.psA()
        for k in range(8):
            kb.mm(p[0:32, 0:N], wgr[:, k, 0:32], self.hb[:, k, 0:N], start=(k == 0), stop=(k == 7))
        kb.copy("act", self.gr_bf[0:32, 0:N], p[0:32, 0:N])
        import os
        ksub = int(os.environ.get("KSUB", "99"))
        if ksub < 2:
            return
        wqk = self.w_get(("in", C_GQ, 512))
        smask = self.smask_p if prompt else self.smask_s
        qv = self.qt[:, :, 0:N].rearrange("p (hp e) n -> p hp e n", e=2)
        kb.memset("pool", qv[64:128, :, 0, :], 0.0)
        kb.memset("pool", qv[0:64, :, 1, :], 0.0)
        eL = []
        for hp in range(2):
            p = self.psA()
            kb.mm(p[:, 0:N], self.wa2[0:16, hp * 128:(hp + 1) * 128], self.gr_bf[0:16, 0:N])
            e = self.tf()
            kb.act(e[:, 0:N], p[:, 0:N], AF.Exp, bias=self.dcst[:, 16 + hp:17 + hp], scale=-1.0)
            kb.act(e[:, 0:N], e[:, 0:N], AF.Ln, bias=1.0)
            la = e
            if prompt:
                kb.ts("dve", la[:, 0:N], e[:, 0:N], -1.0 / 16.0, None, ALU.mult)
            else:
                kb.tt("dve", la[:, 0:N], e[:, 0:N], self.negv_s[:, 0:N], ALU.mult)
            b = self.tf()
            kb.scan(b[:, 0:N], smask[:, 0:N], la[:, 0:N], 0.0, ALU.mult, ALU.add)
            Eb = self.tf()
            kb.act(Eb[:, 0:N], b[:, 0:N], AF.Exp)
            Enb = self.tf()
            kb.act(Enb[:, 0:N], b[:, 0:N], AF.Exp, scale=-1.0)
            pq = self.proj_fm(wqk, hp * 128, N)
            kb.stt("dve", self.qt[0:64, 2 * hp, 0:N], pq[0:64, 0:N], 0.125, Eb[0:64, 0:N], ALU.mult, ALU.mult)
            kb.stt("dve", self.qt[64:128, 2 * hp + 1, 0:N], pq[64:128, 0:N], 0.125, Eb[64:128, 0:N], ALU.mult, ALU.mult)
            pk = self.proj_fm(wqk, 256 + hp * 128, N)
            kb.tt("dve", self.kt[:, hp, 0:N], pk[:, 0:N], Enb[:, 0:N], ALU.mult)
            eLt = self.eL[hp]
            for c in range(nchunk):
                kb.copy("dve", eLt[:, c:c + 1], Eb[:, c * L + L - 1:c * L + L])
            eL.append(eLt)
        if ksub < 3:
            return
        wgv = self.w_get(("in", C_GV, 512))
        for c in range(nchunk):
            p = self.psA()
            for k in range(8):
                kb.mm(p[0:L, :], self.hb[:, k, c * L:(c + 1) * L], wgv[:, k, :], start=(k == 0), stop=(k == 7))
            dst = self.gv_tok[0:L, c, :] if prompt else self.gvs_tok[0:L, c, :]
            kb.copy("act", dst, p[0:L, :])

        wcache = {}

        def wblk(key, spec):
            if key not in wcache:
                wcache[key] = self.w_get(spec)
            return wcache[key]

        def unit_gg(h):
            w = wblk("gg", ("in", C_GG, 512))
            p = self.proj_fm(w, h * 128, N, use_psB=True)
            kb.act(self.gg_bf[:, h, 0:N], p[:, 0:N], AF.Silu)

        def unit_dq(h):
            w = wblk("dq", ("in", C_DQ, 512))
            if h == 0:
                kb.memset("pool", self.dq1[64:128, :, 0:N], 0.0)
                kb.memset("pool", self.dq2[0:64, :, 0:N], 0.0)
            p = self.proj_fm(w, h * 128, N, use_psB=True)
            kb.copy("act", self.dq1[0:64, h, 0:N], p[0:64, 0:N])
            kb.copy("act", self.dq2[64:128, h, 0:N], p[64:128, 0:N])

        def unit_dk(h):
            w = wblk("dk", ("in", C_DK, 512))
            p = self.proj_fm(w, h * 128, N, use_psB=True)
            t = self.tf()
            kb.copy("act", t[:, 0:N], p[:, 0:N])
            kb.dma("sp", self.kT_out[h * 128:(h + 1) * 128, tok0:tok0 + N], t[:, 0:N], is_output=True)
            if prompt:
                kb.copy("dve", self.KT[:, h, tok0:tok0 + N], p[:, 0:N])
            else:
                kb.copy("dve", self.KT_s[:, h, 0:N], p[:, 0:N])

        def unit_dv(c):
            w = wblk("dv", ("in", C_DV, 512))
            p = self.psB()
            for k in range(8):
                kb.mm(p[0:L, :], self.hb[:, k, c * L:(c + 1) * L], w[:, k, :], start=(k == 0), stop=(k == 7))
            t = self.vtok_f[self.vtf_i % 1]
            self.vtf_i += 1
            kb.copy("act", t[0:L, :], p[0:L, :])
            kb.dma("sp", self.v_out[tok0 + c * L:tok0 + (c + 1) * L, :], t[0:L, :], is_output=True)
            if prompt:
                kb.copy("dve", self.VS[:, (tok0 // 128) + c, :], p[:, :])
            else:
                kb.copy("dve", self.dvs_tok[0:L, c, :], p[0:L, :])

        units = [(unit_gg, i) for i in range(4)] + [(unit_dq, i) for i in range(4)] + \
                [(unit_dk, i) for i in range(4)] + [(unit_dv, i) for i in range(nchunk)]
        if ksub < 7:
            return
        if self.stage < 4:
            kb.memset("dve", self.merge[:, 4:8, :], 0.0)
        self.gla(tok0, N, L, nchunk, prompt, eL, units)
        if self.stage >= 4:
            if prompt:
                last_finish = self.attn_prompt(tok0, N)
            else:
                kb.memset("pool", self.merge[:, 4:8, 0:N], 0.0)
                last_finish = self.attn_sample(N)
        else:
            last_finish = (lambda: None)
        wo = self.w_get(("out", 0))
        banks = [self.pb[0], self.pb[1], self.pb[2], self.pb[3]]
        for k in range(7):
            for ml in range(4):
                kb.mm(banks[ml][:, 0:N], wo[:, k, ml * 128:(ml + 1) * 128], self.merge[:, k, 0:N], start=(k == 0), stop=False)
        last_finish()
        for ml in range(4):
            kb.mm(banks[ml][:, 0:N], wo[:, 7, ml * 128:(ml + 1) * 128], self.merge[:, 7, 0:N], start=False, stop=True)
            kb.act(self.hb[:, ml, 0:N], banks[ml][:, 0:N], AF.Square)
            kb.copy("act", self.f[:, ml, 0:N], banks[ml][:, 0:N])
        wo = self.w_get(("out", 1))
        for ml in range(4):
            p = self.psB()
            for k in range(8):
                kb.mm(p[:, 0:N], wo[:, k, ml * 128:(ml + 1) * 128], self.merge[:, k, 0:N], start=(k == 0), stop=(k == 7))
            kb.act(self.hb[:, 4 + ml, 0:N], p[:, 0:N], AF.Square)
            kb.copy("act", self.f[:, 4 + ml, 0:N], p[:, 0:N])
        self.residual_update(N, self.cst, K_MPOST)

    def gla(self, tok0, N, L, nchunk, prompt, eL, units=()):
        kb = self.kb
        units = list(units)
        gv = self.gv_tok if prompt else self.gvs_tok

        def stage1(c):
            cols = slice(c * L, (c + 1) * L)
            pt = self.psA()
            ptb = pt[:].bitcast(BF16)
            for hp in range(2):
                kb.transpose(ptb[0:L, hp * 128:(hp + 1) * 128], self.kt[:, hp, cols], self.ident[:, :])
            ktk = self.tb()
            kb.copy("dve", ktk[0:L, 0:256], ptb[0:L, 0:256])
            pab = (self.psA(), self.psA())
            for h in range(4):
                hp = h // 2
                rows = slice((h % 2) * 64, (h % 2) * 64 + 64)
                kb.mm(pab[h % 2][0:L, h * 128:h * 128 + L], self.kt[:, hp, cols], self.qt[:, h, cols])
            at = self.tb()
            for h in range(4):
                kb.tt("dve", at[0:L, h * 128:h * 128 + L], pab[h % 2][0:L, h * 128:h * 128 + L], self.cmask[0:L, 0:L], ALU.mult)
            return at, ktk

        def stage2(c, at, ktk):
            cols = slice(c * L, (c + 1) * L)
            if not prompt:
                kb.dma("sp", self.S[:], self.state_s[c])
                kb.copy("dve", self.S_bf[:], self.S[:])
            po = self.psA()
            pS = self.psA()
            for h in range(4):
                hp = h // 2
                rows = slice((h % 2) * 64, (h % 2) * 64 + 64)
                kb.mm(po[:, h * 128:h * 128 + L], gv[0:L, c, h * 128:(h + 1) * 128], at[0:L, h * 128:h * 128 + L],
                      start=True, stop=False)
                kb.mm(po[:, h * 128:h * 128 + L], self.S_bf[:, hp, :], self.qt[:, h, cols], start=False, stop=True)
            for h in range(4):
                hp = h // 2
                kb.mm(pS[:, h * 128:(h + 1) * 128], ktk[0:L, hp * 128:(hp + 1) * 128], gv[0:L, c, h * 128:(h + 1) * 128])
            for h in range(4):
                kb.copy("act", self.o_sb[:, h, cols], po[:, h * 128:h * 128 + L])
            t = self.tf()
            pSv = pS[:, 0:512].rearrange("p (hp e v) -> p hp e v", hp=2, e=2)
            tv = t[:, 0:256].rearrange("p (hp v) -> p hp v", hp=2)
            kb.tt("dve", tv[0:64, :, :], pSv[0:64, :, 0, :], self.S[0:64, :, :], ALU.add)
            kb.tt("dve", tv[64:128, :, :], pSv[64:128, :, 1, :], self.S[64:128, :, :], ALU.add)
            for hp in range(2):
                kb.ts("dve", self.S[:, hp, :], tv[:, hp, :], eL[hp][:, c:c + 1], None, ALU.mult)
            kb.copy("dve", self.S_bf[:], self.S[:])
            if not prompt:
                kb.dma("sp", self.S_s[c], self.S[:], is_output=True)

        cur = stage1(0)
        for c in range(nchunk):
            nxt = stage1(c + 1) if c + 1 < nchunk else None
            stage2(c, *cur)
            for _ in range(4):
                if units:
                    fn, arg = units.pop(0)
                    fn(arg)
            cur = nxt
        while units:
            fn, arg = units.pop(0)
            fn(arg)
        pss = [self.psB() for h in range(4)]
        rts = [self.tf() for h in range(4)]
        for h in range(4):
            kb.act(self.hb[:, h, 0:N], self.o_sb[:, h, 0:N], AF.Square)
        for h in range(4):
            kb.mm(pss[h][:, 0:N], self.ones_bf[:, :], self.hb[:, h, 0:N])
        for h in range(4):
            kb.act(rts[h][:, 0:N], pss[h][:, 0:N], AF.Ln, bias=EPS, scale=1.0 / 128)
        for h in range(4):
            kb.act(rts[h][:, 0:N], rts[h][:, 0:N], AF.Exp, scale=-0.5)
        for h in range(4):
            oh = self.o_sb[:, h, 0:N]
            kb.stt("dve", oh, oh, self.cst[:, K_GLAG:K_GLAG + 1], rts[h][:, 0:N], ALU.mult, ALU.mult)
            kb.tt("dve", self.merge[:, h, 0:N], oh, self.gg_bf[:, h, 0:N], ALU.mult)

    def attn_core(self, N, h, q_aps, ktiles, hook=None, G=1):
        kb = self.kb
        O = (self.pb[4], self.pb[5])
        Ls = (self.pb[6], self.pb[7])
        nt = len(ktiles)
        groups = []
        j = 0
        while j < nt:
            if G > 1 and ktiles[j][2] == 128 and ktiles[j][3] == 128 and not ktiles[j][5]:
                g = [j]
                while len(g) < G and g[-1] + 1 < nt and ktiles[g[-1] + 1][2] == 128 and ktiles[g[-1] + 1][3] == 128 \
                        and not ktiles[g[-1] + 1][5]:
                    g.append(g[-1] + 1)
                groups.append(g)
                j = g[-1] + 1
            else:
                groups.append([j])
                j += 1

        def qk(grp):
            pss = []
            for c in range(2):
                ps = self.psA()
                for gi, j in enumerate(grp):
                    KTa, Va, nkM, nk, q0, diag = ktiles[j]
                    if len(grp) == 1:
                        kb.mm(ps[0:nkM, q0:N], KTa, q_aps[c][:, q0:N])
                    else:
                        kb.mm(ps[0:nkM, gi * N:(gi + 1) * N], KTa, q_aps[c][:, 0:N])
                pss.append(ps)
            return pss

        def rest(grp, pss):
            for c in range(2):
                ps = pss[c]
                P = self.tb()
                if len(grp) == 1:
                    j = grp[0]
                    KTa, Va, nkM, nk, q0, diag = ktiles[j]
                    kb.act(P[0:nk, q0:N], ps[0:nk, q0:N], AF.Exp, scale=0.125)
                    if diag:
                        kb.memset("dve", P[64:128, q0:q0 + 64], 0.0)
                    kb.mm(O[c][:, q0:N], Va, P[0:nk, q0:N], start=(j == 0), stop=(j == nt - 1))
                    kb.mm(Ls[c][:, q0:N], self.ones_bf[0:nk, :], P[0:nk, q0:N], start=(j == 0), stop=(j == nt - 1))
                else:
                    W = len(grp) * N
                    kb.act(P[:, 0:W], ps[:, 0:W], AF.Exp, scale=0.125)
                    for gi, j in enumerate(grp):
                        Va = ktiles[j][1]
                        kb.mm(O[c][:, 0:N], Va, P[:, gi * N:(gi + 1) * N], start=(j == 0), stop=(j == nt - 1))
                    for gi, j in enumerate(grp):
                        kb.mm(Ls[c][:, 0:N], self.ones_bf[:, :], P[:, gi * N:(gi + 1) * N], start=(j == 0), stop=(j == nt - 1))

        prev = None
        for gidx, grp in enumerate(groups):
            pss = qk(grp)
            if prev is not None:
                rest(*prev)
            prev = (grp, pss)
            if hook is not None and gidx == min(2, len(groups) - 1):
                hook()
                hook = None
        rest(*prev)
        if hook is not None:
            hook()
        r1 = self.tf()
        r2 = self.tf()
        t1 = self.tf()
        t2 = self.tf()
        kb.recip(r1[:, 0:N], Ls[0][:, 0:N])
        kb.copy("act", t1[:, 0:N], O[0][:, 0:N])
        kb.recip(r2[:, 0:N], Ls[1][:, 0:N])
        kb.copy("act", t2[:, 0:N], O[1][:, 0:N])
        return (r1, r2, t1, t2)

    def attn_finishA(self, ev, N, h):
        kb = self.kb
        r1, r2, t1, t2 = ev
        kb.tt("dve", t1[:, 0:N], t1[:, 0:N], r1[:, 0:N], ALU.mult)
        kb.tt("dve", t2[:, 0:N], t2[:, 0:N], r2[:, 0:N], ALU.mult)
        od = self.o_sb[:, h, 0:N]
        kb.stt("dve", od, t2[:, 0:N], self.neg_lam, t1[:, 0:N], ALU.mult, ALU.add)
        kb.act(self.hb[:, 0, 0:N], od, AF.Square)

    def attn_finishB(self, N, h, cols=None, bank=None):
        kb = self.kb
        od = self.o_sb[:, h, 0:N]
        if bank is not None:
            p = bank
        else:
            p = self.psA()
            if self.pa_i % 2 == 1:
                self.pa_i += 1
        kb.mm(p[:, 0:N], self.ones_bf[:, :], self.hb[:, 0, 0:N])
        kb.act(self.rstd[:, 0:N], p[:, 0:N], AF.Ln, bias=EPS, scale=1.0 / 128)
        kb.act(self.rstd[:, 0:N], self.rstd[:, 0:N], AF.Exp, scale=-0.5)
        dst = self.merge[:, 4 + h, 0:N] if cols is None else self.merge[:, 4 + h, cols]
        kb.stt("dve", dst, od, self.dg08, self.rstd[:, 0:N], ALU.mult, ALU.mult)

    def attn_prompt(self, tok0, N):
        i = tok0 // NTK
        pending = None
        for h in range(4):
            ktiles = []
            for j in range(4 * i + 4):
                m = j - 4 * i
                q0 = 0 if m < 0 else 128 * m
                KTa = self.KT[:, h, j * 128:(j + 1) * 128]
                Va = self.VS[:, j, h * 128:(h + 1) * 128]
                ktiles.append((KTa, Va, 128, 128, q0, m >= 0))
            hook = None
            if pending is not None:
                pe, ph = pending
                hook = (lambda pe=pe, ph=ph: self.attn_finishA(pe, N, ph))
            ev = self.attn_core(N, h, (self.dq1[:, h, 0:N], self.dq2[:, h, 0:N]), ktiles, hook)
            if pending is not None:
                self.attn_finishB(N, pending[1])
            pending = (ev, h)
        return (lambda: (self.attn_finishA(pending[0], N, pending[1]), self.attn_finishB(N, pending[1], bank=self.pb[4])))

    def cache_prefetch(self, upto):
        while self.cache_loaded <= min(upto, 4 * NS - 1):
            i = self.cache_loaded
            h, s_ = i // NS, i % NS
            self.kb.dma("pool", self.kc_sb[i % self.NCACHE][:], self.kTc[s_, h])
            self.kb.dma("pool", self.vc_sb[i % self.NCACHE][:], self.vc[s_, :, h, :].rearrange("(j p) v -> p j v", p=128))
            self.cache_loaded += 1

    def attn_sample(self, N):
        kb = self.kb
        pending = None
        for h in range(4):
            for s in range(NS):
                i = self.cache_i
                self.cache_i += 1
                kc = self.kc_sb[i % self.NCACHE]
                vcs = self.vc_sb[i % self.NCACHE]
                self.cache_prefetch(i + self.NCACHE - 1)
                cols = slice(s * TSP, s * TSP + TS)
                colsM = slice(s * TSP, (s + 1) * TSP)
                ktiles = []
                for j in range(PAST // 128):
                    ktiles.append((kc[:, j * 128:(j + 1) * 128], vcs[:, j, :], 128, 128, 0, False))
                ktiles.append((self.KT_s[:, h, colsM], self.dvs_tok[0:TS, s, h * 128:(h + 1) * 128], TSP, TS, 0, False))
                hook = None
                if pending is not None:
                    pe, ph, pc = pending
                    hook = (lambda pe=pe, ph=ph: self.attn_finishA(pe, TS, ph))
                ev = self.attn_core(TS, h, (self.dq1[:, h, cols], self.dq2[:, h, cols]), ktiles, hook, G=8)
                if pending is not None:
                    self.attn_finishB(TS, pending[1], pending[2])
                pending = (ev, h, cols)
        return (lambda: (self.attn_finishA(pending[0], TS, pending[1]), self.attn_finishB(TS, pending[1], pending[2], bank=self.pb[4])))

    def load_x(self, tok0, N):
        for c in range(8):
            self.kb.dma("pool", self.x2[:, c, 0:N], self.xT[c * 128:(c + 1) * 128, tok0:tok0 + N])

    def tile(self, tok0, N, prompt, first=False, last=False, nxt=None):
        kb = self.kb
        self.cur_tok0 = tok0
        self.stage = 9
        if not prompt:
            self.cache_prefetch(self.NCACHE - 2)
        self.ffn(1, N, xsrc=self.x2)
        self.mixer(tok0, N, prompt)
        if nxt is not None:
            self.load_x(nxt[0], nxt[1])
        self.ffn_prenorm(2, N, self.x)
        hook = None
        if nxt is not None:
            hook = (lambda: self.ffn_prenorm(1, nxt[1], self.x2))
        self.ffn(2, N, xsrc=self.x, mid_hook=hook)
        if prompt and last:
            kb.dma("sp", self.S_p, self.S[:], is_output=True)


_PROG_CACHE = {}


def _get_prog(n_tiles=8, do_sample=True):
    key = (n_tiles, do_sample)
    if key not in _PROG_CACHE:
        nc = bass.Bass("TRN2", target_bir_lowering=False)
        Prog(nc, n_tiles=n_tiles, do_sample=do_sample)
        _PROG_CACHE[key] = nc
    return _PROG_CACHE[key]


def _lay_wgr(w_in):
    g = np.zeros((128, 8, 32), np.float32)
    g[:, :, 0:16] = np.asarray(w_in[:, C_GR:C_GR + 16], np.float32).reshape(8, 128, 16).transpose(1, 0, 2)
    return np.ascontiguousarray(g.reshape(128, 256))


def _pack_consts(inp):
    c = np.zeros((128, NCST), np.float32)
    def g8(v):
        return np.ascontiguousarray(np.asarray(v, np.float32).reshape(8, 128).T)
    c[:, K_F1PRE:K_F1PRE + 8] = g8(inp["ffn1_pre_g"][0])
    c[:, K_F1POST:K_F1POST + 8] = g8(inp["ffn1_post_g"][0])
    c[:, K_MPRE:K_MPRE + 8] = g8(inp["mix_pre_g"][0])
    c[:, K_MPOST:K_MPOST + 8] = g8(inp["mix_post_g"][0])
    c[:, K_F2PRE:K_F2PRE + 8] = g8(inp["ffn2_pre_g"][0])
    c[:, K_F2POST:K_F2POST + 8] = g8(inp["ffn2_post_g"][0])
    c[:, K_BA:K_BA + 2] = np.asarray(inp["b_gate_a"][0], np.float32).reshape(2, 128).T
    c[:, K_GLAG] = inp["gla_norm_g"][0]
    c[:, K_DIFFG] = inp["diff_norm_g"][0]
    for i, k in enumerate(("lambda_q1", "lambda_k1", "lambda_q2", "lambda_k2")):
        c[0:64, K_LAM + i] = inp[k][0]
    return c


def _run(inp, n_tiles=8, do_sample=True):
    import os
    do_sample = do_sample and os.environ.get("KNOSAMPLE", "0") != "1"
    nc = _get_prog(n_tiles, do_sample)
    T = n_tiles * NTK
    f32 = lambda a: np.ascontiguousarray(np.asarray(a, np.float32))
    cst = _pack_consts(inp)
    ident = np.eye(128, dtype=np.float32)
    cmask = np.triu(np.ones((128, 128), np.float32))
    shared = {
        "cst": cst, "w_in": f32(inp["w_in"][0]), "w_a2": f32(inp["w_gate_a2"][0]), "w_out": f32(inp["w_out"][0]),
        "f1g": f32(inp["ffn1_w_gate"][0]), "f1u": f32(inp["ffn1_w_up"][0]), "f1d": f32(inp["ffn1_w_down"][0]),
        "f2g": f32(inp["ffn2_w_gate"][0]), "f2u": f32(inp["ffn2_w_up"][0]), "f2d": f32(inp["ffn2_w_down"][0]),
        "ident": ident, "cmask": cmask,
        "w_gr": _lay_wgr(inp["w_in"][0]),
    }
    in_maps = []
    for c in range(8):
        xp = np.asarray(inp["x_prompt"][c][:T], np.float32).T
        xs = np.zeros((NS, TSP, D), np.float32)
        xs[:, :TS] = np.asarray(inp["x_sample"][NS * c:NS * c + NS], np.float32)
        xs = xs.reshape(NS * TSP, D).T
        xT = np.ascontiguousarray(np.concatenate([xp, xs], axis=1)) if do_sample else np.ascontiguousarray(xp)
        if not do_sample:
            m = dict(shared)
            m["xT"] = xT
            in_maps.append(m)
            continue
        st = np.asarray(inp["state_gla"][0, NS * c:NS * c + NS], np.float32)
        st = st.reshape(NS, 2, 2, 64, 128).transpose(0, 2, 3, 1, 4).reshape(NS, 128, 2, 128)
        kTc = np.asarray(inp["cache_diff_k"][0, NS * c:NS * c + NS], np.float32).transpose(0, 2, 3, 1)
        vc = np.asarray(inp["cache_diff_v"][0, NS * c:NS * c + NS], np.float32)
        m = dict(shared)
        m.update({"xT": xT, "state_s": f32(st), "kTc": f32(kTc), "vc": f32(vc)})
        in_maps.append(m)
    res = run_bass_kernel_spmd(nc, in_maps, core_ids=list(range(8)))
    if not do_sample:
        for r in res.results:
            r["yT"] = np.concatenate([r["yT"], np.zeros((D, NS * TSP), np.float32)], axis=1)
            r["kT_out"] = np.concatenate([r["kT_out"], np.zeros((512, NS * TSP), np.float32)], axis=1)
            r["v_out"] = np.concatenate([r["v_out"], np.zeros((NS * TSP, 512), np.float32)], axis=0)
            r["S_s"] = np.zeros((NS, 128, 2, 128), np.float32)
    return res.results, T


def _assemble(results, T):
    y_p = np.zeros((8, T, D), np.float32)
    y_s = np.zeros((8 * NS, TS, D), np.float32)
    sg_p = np.zeros((1, 8, 4, 64, 128), np.float32)
    k_p = np.zeros((1, 8, T, 4, 128), np.float32)
    v_p = np.zeros((1, 8, T, 4, 128), np.float32)
    sg_s = np.zeros((1, 8 * NS, 4, 64, 128), np.float32)
    k_s = np.zeros((1, 8 * NS, TS, 4, 128), np.float32)
    v_s = np.zeros((1, 8 * NS, TS, 4, 128), np.float32)
    unS = lambda s: s.reshape(2, 64, 2, 128).transpose(2, 0, 1, 3).reshape(4, 64, 128)
    for c in range(8):
        r = results[c]
        yT = r["yT"]
        y_p[c] = yT[:, :T].T
        y_s[NS * c:NS * c + NS] = yT[:, T:].T.reshape(NS, TSP, D)[:, :TS]
        kT = r["kT_out"]
        k_p[0, c] = kT[:, :T].T.reshape(T, 4, 128)
        k_s[0, NS * c:NS * c + NS] = kT[:, T:].T.reshape(NS, TSP, 4, 128)[:, :TS]
        vo = r["v_out"]
        v_p[0, c] = vo[:T].reshape(T, 4, 128)
        v_s[0, NS * c:NS * c + NS] = vo[T:].reshape(NS, TSP, 4, 128)[:, :TS]
        sg_p[0, c] = unS(r["S_p"])
        for s in range(NS):
            sg_s[0, NS * c + s] = unS(r["S_s"][s])
    return (y_p, y_s, sg_p, k_p, v_p, sg_s, k_s, v_s)


def kernel(**inputs):
    results, T = _run(inputs, 8, True)
    return _assemble(results, T)
```

```python
import concourse.bass as bass
import concourse.mybir as mybir

F32 = mybir.dt.float32
BF16 = mybir.dt.bfloat16
ALU = mybir.AluOpType
AF = mybir.ActivationFunctionType
PG = 64

_DSZ = {F32: 4, BF16: 2, mybir.dt.int32: 4, mybir.dt.uint32: 4, mybir.dt.float16: 2,
        mybir.dt.uint16: 2, mybir.dt.int16: 2, mybir.dt.uint8: 1, mybir.dt.int8: 1}


class Op:
    __slots__ = ("eng", "fn", "deps", "idx", "tick", "sem", "semval", "is_dma", "need_inc", "name")

    def __init__(self, eng, fn, is_dma=False, name=""):
        self.eng = eng
        self.fn = fn
        self.deps = []
        self.is_dma = is_dma
        self.need_inc = False
        self.tick = None
        self.sem = None
        self.semval = None
        self.name = name


class Res:
    __slots__ = ("w", "r")

    def __init__(self):
        self.w = None
        self.r = []


class KB:
    ENGS = ("pe", "act", "dve", "pool", "sp")

    def __init__(self, nc, n_dma_sems=(24, 40, 2)):
        self.nc = nc
        self.ops = {e: [] for e in self.ENGS}
        self.res = {}
        self.tinfo = {}
        self.sb_off = 16640
        self.ps_off = 0
        self.n_dma_sems = dict(sp=n_dma_sems[0], pool=n_dma_sems[1], act=n_dma_sems[2])
        self.dma_hist = {"sp": [], "pool": [], "act": []}
        self.out_dmas = []
        self.nops = 0

    def sb(self, name, shape, dtype, at=None):
        nbytes = _DSZ[dtype]
        for s in shape[1:]:
            nbytes *= s
        if at is None:
            at = (self.sb_off + 63) // 64 * 64
            self.sb_off = at + nbytes
        t = self.nc.alloc_sbuf_tensor_at(name, list(shape), dtype, offset=at)
        self.tinfo[t.name] = ("S", at, nbytes)
        return t

    def sb_mark(self):
        return self.sb_off

    def sb_reset(self, mark):
        self.sb_off = mark

    def ps(self, name, shape, dtype=F32):
        nbytes = _DSZ[dtype]
        for s in shape[1:]:
            nbytes *= s
        t = self.nc.alloc_psum_tensor(name, list(shape), dtype)
        nb = (nbytes + 2047) // 2048
        self.tinfo[t.name] = ("P", self.ps_off, nb * 2048)
        self.ps_off += nb * 2048
        return t

    def dram_token(self, name):
        return ("D", name)

    def pages(self, ap):
        if isinstance(ap, tuple):
            return [ap]
        name = ap.tensor.name
        info = self.tinfo.get(name)
        if info is None:
            return []
        space, base, bpp = info
        dsz = _DSZ[ap.dtype]
        pstep = bpp // dsz
        apl = ap.ap
        fo = ap.offset % pstep
        ext = 0
        for (s, c) in apl[1:]:
            if s > 0:
                ext += (c - 1) * s
        lo = base + fo * dsz
        hi = base + (fo + ext + 1) * dsz - 1
        return [(space, p) for p in range(lo // PG, hi // PG + 1)]

    def _add(self, op, reads, writes):
        rp = set()
        for a in reads:
            rp.update(self.pages(a))
        wp = set()
        for a in writes:
            wp.update(self.pages(a))
        for p in list(rp) + list(wp):
            if p[0] == "P":
                wp.add(("PB", p[1] // (2048 // PG)))
        deps = set()
        for p in rp:
            r = self.res.get(p)
            if r is None:
                r = self.res[p] = Res()
            w = r.w
            if w is not None:
                if w.eng == op.eng and not w.is_dma and not op.is_dma and op.eng == "pe":
                    pass
                else:
                    deps.add(w)
        for p in wp:
            r = self.res.get(p)
            if r is None:
                r = self.res[p] = Res()
            w = r.w
            if w is not None:
                if w.eng == op.eng and not w.is_dma and not op.is_dma and op.eng == "pe":
                    pass
                else:
                    deps.add(w)
            for rd in r.r:
                if rd is op:
                    continue
                if rd.eng == op.eng and not rd.is_dma and not op.is_dma and op.eng == "pe":
                    continue
                deps.add(rd)
        for p in rp:
            r = self.res[p]
            if op.is_dma:
                r.r.append(op)
            else:
                r.r = [x for x in r.r if x.is_dma or x.eng != op.eng]
                r.r.append(op)
        for p in wp:
            r = self.res[p]
            r.w = op
            r.r = []
        deps.discard(op)
        op.deps = list(deps)
        op.idx = self.nops
        self.nops += 1
        self.ops[op.eng].append(op)
        return op

    def op(self, eng, fn, reads=(), writes=(), name=""):
        return self._add(Op(eng, fn, False, name), reads, writes)

    def dma(self, q, out, in_, extra_reads=(), extra_writes=(), is_output=False, name=""):
        o = Op(q, lambda e, out=out, in_=in_: e.dma_start(out=out, in_=in_), True, name)
        hist = self.dma_hist[q]
        n = self.n_dma_sems[q]
        k = len(hist)
        o.sem = (q, k % n)
        o.semval = 16 * (k // n + 1)
        self._add(o, [in_] + list(extra_reads), [out] + list(extra_writes))
        if k >= n:
            prev = hist[k - n]
            if prev not in o.deps:
                o.deps.append(prev)
        hist.append(o)
        if is_output:
            self.out_dmas.append(o)
        return o

    def mm(self, out, lhsT, rhs, start=True, stop=True, **kw):
        return self.op("pe", lambda e: e.matmul(out, lhsT, rhs, start=start, stop=stop, **kw),
                       [lhsT, rhs] + ([] if start else []), [out])

    def transpose(self, out, in_, ident):
        return self.op("pe", lambda e: e.transpose(out, in_, ident), [in_, ident], [out])

    def act(self, out, in_, func, bias=None, scale=1.0, accum_out=None, eng="act"):
        reads = [in_]
        kw = {}
        if bias is not None:
            kw["bias"] = bias
            if not isinstance(bias, (int, float)):
                reads.append(bias)
        if not isinstance(scale, (int, float)):
            reads.append(scale)
        writes = [out]
        if accum_out is not None:
            kw["accum_out"] = accum_out
            writes.append(accum_out)
        return self.op("act", lambda e: e.activation(out, in_, func, scale=scale, **kw), reads, writes)

    def tt(self, eng, out, in0, in1, op):
        return self.op(eng, lambda e: e.tensor_tensor(out, in0, in1, op), [in0, in1], [out])

    def ts(self, eng, out, in0, s1, s2, op0, op1=None):
        reads = [in0]
        if not isinstance(s1, (int, float)):
            reads.append(s1)
        if s2 is not None and not isinstance(s2, (int, float)):
            reads.append(s2)
        if op1 is None:
            s2 = 0.0
            op1 = ALU.add
        return self.op(eng, lambda e: e.tensor_scalar(out, in0, s1, s2, op0, op1), reads, [out])

    def stt(self, eng, out, in0, scalar, in1, op0, op1):
        reads = [in0, in1]
        if not isinstance(scalar, (int, float)):
            reads.append(scalar)
        return self.op(eng, lambda e: e.scalar_tensor_tensor(out, in0, scalar, in1, op0, op1), reads, [out])

    def copy(self, eng, out, in_):
        if eng == "act":
            return self.op("act", lambda e: e.copy(out, in_), [in_], [out])
        return self.op(eng, lambda e: e.tensor_copy(out, in_), [in_], [out])

    def memset(self, eng, out, val):
        return self.op(eng, lambda e: e.memset(out, val), [], [out])

    def recip(self, out, in_):
        return self.op("dve", lambda e: e.reciprocal(out, in_), [in_], [out])

    def scan(self, out, d0, d1, init, op0, op1, eng="dve"):
        reads = [d0, d1]
        if not isinstance(init, (int, float)):
            reads.append(init)
        return self.op(eng, lambda e: e.tensor_tensor_scan(out, d0, d1, init, op0, op1), reads, [out])

    def emit(self):
        nc = self.nc
        for e in self.ENGS:
            for o in self.ops[e]:
                for d in o.deps:
                    d.need_inc = True
        for o in self.out_dmas:
            o.need_inc = True
        for q in self.dma_hist:
            for o in self.dma_hist[q]:
                o.need_inc = True
        for e in ("pe", "act", "dve", "pool"):
            t = 0
            for o in self.ops[e]:
                if o.is_dma:
                    continue
                if o.need_inc:
                    t += 1
                    o.tick = t
        sems = {}
        import contextlib
        with contextlib.ExitStack() as st:
            for e in ("pe", "act", "dve", "pool"):
                sems[e] = st.enter_context(nc.semaphore("c_" + e))
            for q in ("sp", "pool", "act"):
                for i in range(self.n_dma_sems[q]):
                    sems[(q, i)] = st.enter_context(nc.semaphore(f"d_{q}{i}"))
            block = st.enter_context(nc.Block())

            def gen(ename):
                def body(eng):
                    waited = {}
                    for o in self.ops[ename]:
                        need = {}
                        for d in o.deps:
                            if d.is_dma:
                                key, val = d.sem, d.semval
                            else:
                                key, val = d.eng, d.tick
                            if waited.get(key, 0) >= val:
                                continue
                            if need.get(key, 0) < val:
                                need[key] = val
                        for key, val in need.items():
                            eng.wait_ge(sems[key], val)
                            waited[key] = val
                        ins = o.fn(eng)
                        if o.is_dma:
                            ins.then_inc(sems[o.sem], 16)
                        elif o.need_inc:
                            ins.then_inc(sems[o.eng], 1)
                    if ename == "sp":
                        need = {}
                        alld = self.out_dmas + [d for q in self.dma_hist for d in self.dma_hist[q]]
                        for d in alld:
                            if need.get(d.sem, 0) < d.semval:
                                need[d.sem] = d.semval
                        for key, val in need.items():
                            if waited.get(key, 0) < val:
                                eng.wait_ge(sems[key], val)
                return body

            block.tensor(gen("pe"))
            block.scalar(gen("act"))
            block.vector(gen("dve"))
            block.gpsimd(gen("pool"))
            block.sync(gen("sp"))

import numpy as np
from concourse.bass_utils import run_bass_kernel_spmd

D = 1024
DFF = 2816
NJ = 22
T_FULL = 4096
NTK = 512
NS = 4
TS = 16
TSP = 32
PAST = 2048
EPS = 1e-6
LAM_INIT = 0.2
C_GQ, C_GK, C_GV, C_GR, C_GG, C_DQ, C_DK, C_DV = 0, 256, 512, 1024, 1040, 1552, 2064, 2576
K_F1PRE, K_F1POST, K_MPRE, K_MPOST, K_F2PRE, K_F2POST, K_BA, K_GLAG, K_DIFFG, K_LAM = 0, 8, 16, 24, 32, 40, 48, 50, 51, 52
NCST = 56


class Prog:
    def __init__(self, nc, n_tiles=8, do_sample=True):
        self.nc = nc
        self.kb = KB(nc)
        self.dbg_names = set()
        self.n_tiles = n_tiles
        self.do_sample = do_sample
        self.T = n_tiles * NTK
        self.build()

    def dram_in(self, name, shape):
        return self.nc.dram_tensor(name, list(shape), F32, kind="ExternalInput").ap()

    def dram_out(self, name, shape):
        return self.nc.dram_tensor(name, list(shape), F32, kind="ExternalOutput").ap()

    def build(self):
        kb = self.kb
        T = self.T
        TT = T + (NS * TSP if self.do_sample else 0)
        self.xT = self.dram_in("xT", [D, TT])
        self.cst_d = self.dram_in("cst", [128, NCST])
        self.w_in = self.dram_in("w_in", [D, 3088])
        self.w_a2 = self.dram_in("w_a2", [16, 256])
        self.w_gr = self.dram_in("w_gr", [128, 256])
        self.w_out = self.dram_in("w_out", [D, D])
        self.fw = {}
        for f in (1, 2):
            self.fw[f] = (self.dram_in(f"f{f}g", [D, DFF]), self.dram_in(f"f{f}u", [D, DFF]),
                          self.dram_in(f"f{f}d", [DFF, D]))
        if self.do_sample:
            self.state_s = self.dram_in("state_s", [NS, 128, 2, 128])
            self.kTc = self.dram_in("kTc", [NS, 4, 128, PAST])
            self.vc = self.dram_in("vc", [NS, PAST, 4, 128])
        self.ident_d = self.dram_in("ident", [128, 128])
        self.cmask_d = self.dram_in("cmask", [128, 128])
        self.yT = self.dram_out("yT", [D, TT])
        self.kT_out = self.dram_out("kT_out", [512, TT])
        self.v_out = self.dram_out("v_out", [TT, 512])
        self.S_p = self.dram_out("S_p", [128, 2, 128])
        if self.do_sample:
            self.S_s = self.dram_out("S_s", [NS, 128, 2, 128])

        sb = kb.sb
        self.cst = sb("cst_sb", [128, NCST], F32)
        self.dcst = sb("dcst", [128, 32], F32)
        self.ones_bf = sb("ones_bf", [128, 128], BF16)
        self.ident = sb("ident_sb", [128, 128], BF16)
        self.cmask = sb("cmask_sb", [128, 128], BF16)
        self.smask_p = sb("smask_p", [128, NTK], F32)
        self.smask_s = sb("smask_s", [128, NS * TSP], F32)
        self.negv_s = sb("negv_s", [128, NS * TSP], F32)
        self.wa2 = sb("wa2_sb", [16, 256], BF16)
        self.wgr_f = sb("wgr_f", [128, 256], F32)
        self.ones_f = self.wgr_f[:, 0:128]
        self.wgr = sb("wgr", [128, 8, 32], BF16)
        self.KT = sb("KT", [128, 4, T_FULL], BF16)
        self.VS = sb("VS", [128, 32, 512], BF16)
        self.x = sb("x", [128, 8, NTK], F32)
        self.hb = sb("hb", [128, 8, NTK], BF16)
        self.f = sb("f", [128, 8, NTK], F32)
        self.rstd = sb("rstd", [128, NTK], F32)
        self.NSLOT = 5
        self.wring = [sb(f"wr{i}", [128, 4096], BF16) for i in range(self.NSLOT)]
        um = kb.sb_mark()
        self.a = sb("a", [128, NJ, NTK], BF16)
        uend = kb.sb_mark()
        kb.sb_reset(um)
        self.gg_bf = sb("gg_bf", [128, 4, NTK], BF16)
        self.dq1 = sb("dq1", [128, 4, NTK], BF16)
        self.dq2 = sb("dq2", [128, 4, NTK], BF16)
        self.gv_tok = sb("gv_tok", [128, 4, 512], BF16)
        self.qt = sb("qt", [128, 4, NTK], BF16)
        self.kt = sb("kt", [128, 2, NTK], BF16)
        self.o_sb = sb("o_sb", [128, 4, NTK], F32)
        self.merge = sb("merge", [128, 8, NTK], BF16)
        kb.sb_reset(max(uend, kb.sb_mark()))
        self.x2 = sb("x2", [128, 8, NTK], F32, at=kb.tinfo[self.o_sb.name][1])
        assert kb.tinfo[self.merge.name][1] == kb.tinfo[self.o_sb.name][1] + 8192
        self.tfp = [sb(f"tf{i}", [128, NTK], F32) for i in range(4)]
        self.tbp = [sb(f"tb{i}", [128, NTK], BF16) for i in range(4)]
        self.tfi = 0
        self.tbi = 0
        self.S = sb("S", [128, 2, 128], F32)
        self.S_bf = sb("S_bf", [128, 2, 128], BF16)
        self.ktok = sb("ktok", [128, 2, 128], BF16)
        self.at_sb = sb("at_sb", [128, 128], BF16)
        self.gr_bf = sb("gr_bf", [32, NTK], BF16)
        self.eL = [sb(f"eL{i}", [128, 8], F32) for i in range(2)]
        self.cache_i = 0
        self.vtok_f = [sb(f"vtokf{i}", [128, 512], F32) for i in range(1)]
        self.vtf_i = 0
        print('SBUF used', kb.sb_off)
        assert kb.sb_off <= 229312, kb.sb_off
        ktb = kb.tinfo[self.KT.name][1]
        vsb = kb.tinfo[self.VS.name][1]
        self.NCACHE = 6
        self.kc_sb = [sb(f"kc{i}", [128, PAST], BF16, at=ktb + i * 4096) for i in range(self.NCACHE)]
        self.vc_sb = [sb(f"vcs{i}", [128, 16, 128], BF16, at=vsb + i * 4096) for i in range(self.NCACHE)]
        self.KT_s = sb("KT_s", [128, 4, NS * TSP], BF16, at=ktb + 24576)
        self.dvs_tok = sb("dvs_tok", [TSP, NS, 512], BF16, at=ktb + 24576 + 1024)
        self.gvs_tok = sb("gvs_tok", [TSP, NS, 512], BF16, at=vsb + 24576)
        self.cache_loaded = 0
        self.pb = [kb.ps(f"pb{i}", [128, 512], F32) for i in range(8)]
        self.pa_i = 0
        self.pb_i = 0

        self.setup_consts()
        self.blk_ids = {}
        for i, sp_ in enumerate(self.tile_specs()):
            self.blk_ids[sp_] = i
        self.wbf = self.nc.dram_tensor("wbf16", [len(self.blk_ids), 128, 4096], BF16).ap()
        self.converted = set()
        self.conv_next = 0
        self.wq = []
        self.wq_pos = 0
        self.wq_issued = 0
        specs = []
        for t in range(self.n_tiles):
            specs += self.tile_specs()
        if self.do_sample:
            specs += self.tile_specs()
        import os
        st = int(os.environ.get("KSTAGE", "9"))
        if st < 9:
            one = self.tile_specs()
            ntile = self.n_tiles + (1 if self.do_sample else 0)
            if st == 1:
                one = one[:20]
            else:
                one = one[:28]
            specs = one * ntile
        self.wq = specs
        self.load_x(0, NTK)
        self.w_convert_ahead(8)
        for _ in range(self.NSLOT - 2):
            self.w_issue()
        self.ffn_prenorm(1, NTK, self.x2)
        tiles = [(t * NTK, NTK, True) for t in range(self.n_tiles)]
        if self.do_sample:
            tiles.append((T, NS * TSP, False))
        for i, (t0, n_, pr) in enumerate(tiles):
            nxt = (tiles[i + 1][0], tiles[i + 1][1]) if i + 1 < len(tiles) else None
            self.tile(t0, n_, prompt=pr, first=(i == 0), last=(pr and i == self.n_tiles - 1), nxt=nxt)
        kb.emit()

    def dbg(self, name, ap):
        import os
        if os.environ.get("KDBG", "0") != "1" or name in self.dbg_names:
            return
        self.dbg_names.add(name)
        d = self.nc.dram_tensor("dbg_" + name, list(ap.shape), F32, kind="ExternalOutput").ap()
        self.kb.dma("pool", d, ap, is_output=True)

    def tf(self):
        t = self.tfp[self.tfi % len(self.tfp)]
        self.tfi += 1
        return t

    def tb(self):
        t = self.tbp[self.tbi % len(self.tbp)]
        self.tbi += 1
        return t

    def psA(self):
        t = self.pb[self.pa_i % 4]
        self.pa_i += 1
        return t

    def psB(self):
        t = self.pb[4 + self.pb_i % 4]
        self.pb_i += 1
        return t

    def setup_consts(self):
        kb = self.kb
        kb.dma("sp", self.cst[:], self.cst_d)
        kb.dma("pool", self.ident[:], self.ident_d)
        kb.dma("pool", self.cmask[:], self.cmask_d)
        kb.dma("pool", self.wa2[:], self.w_a2)
        kb.memset("dve", self.ones_bf[:], 1.0)
        kb.memset("dve", self.ones_f, 1.0)
        kb.memset("dve", self.smask_p[:], 1.0)
        kb.memset("dve", self.smask_p[:].rearrange("p (c t) -> p c t", t=128)[:, :, 0:1], 0.0)
        kb.memset("dve", self.smask_s[:], 1.0)
        kb.memset("dve", self.smask_s[:].rearrange("p (c t) -> p c t", t=TSP)[:, :, 0:1], 0.0)
        kb.memset("dve", self.negv_s[:], 0.0)
        kb.memset("dve", self.negv_s[:].rearrange("p (c t) -> p c t", t=TSP)[:, :, 0:TS], -1.0 / 16.0)
        kb.memset("dve", self.S[:], 0.0)
        kb.memset("dve", self.S_bf[:], 0.0)
        dc = self.dcst
        c = self.cst
        kb.ts("dve", dc[:, 0:8], c[:, K_F1POST:K_F1POST + 8], 0.5, None, ALU.mult)
        kb.ts("dve", dc[:, 8:16], c[:, K_F2POST:K_F2POST + 8], 0.5, None, ALU.mult)
        kb.ts("dve", dc[:, 16:18], c[:, K_BA:K_BA + 2], -1.0, None, ALU.mult)
        kb.ts("dve", dc[:, 18:19], c[:, K_DIFFG:K_DIFFG + 1], 1.0 - LAM_INIT, None, ALU.mult)
        kb.tt("dve", dc[:, 20:21], c[:, K_LAM:K_LAM + 1], c[:, K_LAM + 1:K_LAM + 2], ALU.mult)
        kb.tt("dve", dc[:, 21:22], c[:, K_LAM + 2:K_LAM + 3], c[:, K_LAM + 3:K_LAM + 4], ALU.mult)
        p = self.psB()
        kb.mm(p[:, 0:2], self.ones_f, dc[:, 20:22])
        kb.act(dc[:, 22:24], p[:, 0:2], AF.Exp)
        kb.tt("dve", dc[:, 24:25], dc[:, 23:24], dc[:, 22:23], ALU.subtract)
        kb.ts("dve", dc[:, 19:20], dc[:, 24:25], -LAM_INIT, None, ALU.add)
        self.neg_lam = dc[:, 19:20]
        kb.dma("sp", self.wgr_f[:], self.w_gr)
        kb.copy("dve", self.wgr[:].rearrange("p c f -> p (c f)"), self.wgr_f[:])
        self.dg08 = dc[:, 18:19]

    def tile_specs(self):
        s = []
        for f in (1, 2):
            ff = []
            for jb in range(6):
                j0 = jb * 4
                j1 = min(NJ, j0 + 4)
                ff.append(("g", f, j0, j1))
                ff.append(("u", f, j0, j1))
            for mb in range(4):
                for jh in range(2):
                    ff.append(("d", f, mb, jh))
            if f == 1:
                s += ff
                s += [("in", C_GQ, 512), ("in", C_GV, 512), ("in", C_GG, 512), ("in", C_DQ, 512),
                      ("in", C_DK, 512), ("in", C_DV, 512), ("out", 0), ("out", 1)]
            else:
                s += ff
        return s

    def w_src_dst(self, spec, slot):
        wr = self.wring[slot]
        kind = spec[0]
        if kind in ("g", "u"):
            _, f, j0, j1 = spec
            w = self.fw[f][0 if kind == "g" else 1]
            n = (j1 - j0) * 128
            src = w.rearrange("(c p) f -> p c f", p=128)[:, :, j0 * 128:j1 * 128]
            dst = wr[:, 0:8 * n].rearrange("p (c f) -> p c f", c=8)
        elif kind == "d":
            _, f, mb, jh = spec
            w = self.fw[f][2]
            src = w.rearrange("(j p) m -> p j m", p=128)[:, jh * 11:(jh + 1) * 11, mb * 256:(mb + 1) * 256]
            dst = wr[:, 0:11 * 256].rearrange("p (j m) -> p j m", j=11)
        elif kind == "in":
            _, c0, n = spec
            src = self.w_in.rearrange("(c p) f -> p c f", p=128)[:, :, c0:c0 + n]
            dst = wr[:, 0:8 * n].rearrange("p (c f) -> p c f", c=8)
        elif kind == "out":
            _, mh = spec
            src = self.w_out.rearrange("(c p) f -> p c f", p=128)[:, :, mh * 512:(mh + 1) * 512]
            dst = wr[:, 0:8 * 512].rearrange("p (c f) -> p c f", c=8)
        return src, dst

    def w_issue(self):
        if self.wq_issued >= len(self.wq):
            return
        i = self.wq_issued
        spec = self.wq[i]
        src, dst = self.w_src_dst(spec, i % self.NSLOT)
        b = self.blk_ids[spec]
        n = 1
        for d_ in dst.shape[1:]:
            n *= d_
        sc3 = self.wbf[b][:, 0:n].rearrange("p (c f) -> p c f", c=dst.shape[1])
        tok = ("D", "wbf%d" % b)
        wr = self.wring[i % self.NSLOT]
        if b not in self.converted:
            self.converted.add(b)
            self.kb.dma("pool", dst, src)
            self.kb.dma("sp", self.wbf[b][:, 0:n], wr[:, 0:n], extra_writes=[tok])
        else:
            self.kb.dma("sp", wr[:, 0:n], self.wbf[b][:, 0:n], extra_reads=[tok])
        self.wq_issued += 1

    def w_convert_ahead(self, upto):
        return
        nb = len(self.blk_ids)
        while self.conv_next < min(upto, nb):
            spec = self.wq[self.conv_next]
            b = self.blk_ids[spec]
            src, dst = self.w_src_dst(spec, 0)
            n = 1
            for d_ in dst.shape[1:]:
                n *= d_
            sc3 = self.wbf[b][:, 0:n].rearrange("p (c f) -> p c f", c=dst.shape[1])
            self.converted.add(b)
            self.kb.dma("pool", sc3, src, extra_writes=[("D", "wbf%d" % b)])
            self.conv_next += 1

    def w_get(self, spec):
        i = self.wq_pos
        assert self.wq[i] == spec, (self.wq[i], spec)
        self.w_convert_ahead(i + 12)
        self.w_issue()
        _, dst = self.w_src_dst(spec, i % self.NSLOT)
        self.wq_pos += 1
        return dst

    def rms_rstd(self, chunks, N, dim, use_psA=False, presq=False):
        kb = self.kb
        n = len(chunks)
        if not presq:
            for c, ch in enumerate(chunks):
                kb.act(self.hb[:, c, 0:N], ch, AF.Square)
        p = self.psA() if use_psA else self.psB()
        for c in range(n):
            kb.mm(p[:, 0:N], self.ones_bf[:, :], self.hb[:, c, 0:N], start=(c == 0), stop=(c == n - 1))
        kb.act(self.rstd[:, 0:N], p[:, 0:N], AF.Ln, bias=EPS, scale=1.0 / dim)
        kb.act(self.rstd[:, 0:N], self.rstd[:, 0:N], AF.Exp, scale=-0.5)
        return self.rstd[:, 0:N]

    def norm_to_hb(self, src, N, gcol):
        kb = self.kb
        r = self.rms_rstd([src[:, c, 0:N] for c in range(8)], N, D)
        for c in range(8):
            g = self.cst[:, gcol + c:gcol + c + 1]
            if False:
                t = self.tf()
                kb.ts("pool", t[:, 0:N], src[:, c, 0:N], g, None, ALU.mult)
                kb.tt("pool", self.hb[:, c, 0:N], t[:, 0:N], r, ALU.mult)
            else:
                kb.stt("dve", self.hb[:, c, 0:N], src[:, c, 0:N], g, r, ALU.mult, ALU.mult)

    def residual_update(self, N, gtile, gcol, final_tok0=None, xsrc=None, mid_hook=None):
        kb = self.kb
        r = self.rms_rstd([self.f[:, c, 0:N] for c in range(8)], N, D, presq=True)
        if mid_hook is not None:
            mid_hook()
        if xsrc is None:
            xsrc = self.x
        for c in range(8):
            fc = self.f[:, c, 0:N]
            g = gtile[:, gcol + c:gcol + c + 1]
            dst = self.x[:, c, 0:N] if final_tok0 is None else fc
            if False:
                kb.ts("pool", fc, fc, g, None, ALU.mult)
                kb.tt("pool", fc, fc, r, ALU.mult)
                kb.tt("pool", dst, self.x[:, c, 0:N], fc, ALU.add)
            else:
                kb.stt("dve", fc, fc, g, r, ALU.mult, ALU.mult)
                kb.tt("dve", dst, xsrc[:, c, 0:N], fc, ALU.add)
            if final_tok0 is not None:
                kb.dma("sp", self.yT[c * 128:(c + 1) * 128, final_tok0:final_tok0 + N], fc, is_output=True)

    def ffn_prenorm(self, f, N, src):
        kb = self.kb
        pre = K_F1PRE if f == 1 else K_F2PRE
        SQ0 = NJ - 8
        for c in range(8):
            kb.act(self.hb[:, c, 0:N], src[:, c, 0:N], AF.Copy, scale=self.cst[:, pre + c:pre + c + 1])
            kb.act(self.a[:, SQ0 + c, 0:N], src[:, c, 0:N], AF.Square)
        p = self.psB()
        for c in range(8):
            kb.mm(p[:, 0:N], self.ones_bf[:, :], self.a[:, SQ0 + c, 0:N], start=(c == 0), stop=(c == 7))
        r2 = self.vtok_f[0]
        kb.act(r2[:, 0:N], p[:, 0:N], AF.Ln, bias=EPS, scale=1.0 / D)
        kb.act(r2[:, 0:N], r2[:, 0:N], AF.Exp, scale=-0.5)

    def ffn(self, f, N, xsrc=None, mid_hook=None):
        kb = self.kb
        r = self.vtok_f[0][:, 0:N]
        for jb in range(6):
            j0 = jb * 4
            j1 = min(NJ, j0 + 4)
            wg = self.w_get(("g", f, j0, j1))
            wu = self.w_get(("u", f, j0, j1))
            for j in range(j0, j1):
                jl = j - j0
                pg = self.psA()
                pu = self.psA()
                for k in range(8):
                    kb.mm(pg[:, 0:N], wg[:, k, jl * 128:(jl + 1) * 128], self.hb[:, k, 0:N], start=(k == 0), stop=(k == 7))
                for k in range(8):
                    kb.mm(pu[:, 0:N], wu[:, k, jl * 128:(jl + 1) * 128], self.hb[:, k, 0:N], start=(k == 0), stop=(k == 7))
                t1 = self.tf()
                kb.tt("dve", t1[:, 0:N], pg[:, 0:N], r, ALU.mult)
                sg = self.tb()
                kb.act(sg[:, 0:N], t1[:, 0:N], AF.Silu)
                t2 = self.tf()
                kb.tt("dve", t2[:, 0:N], pu[:, 0:N], r, ALU.mult)
                kb.tt("dve", self.a[:, j, 0:N], t2[:, 0:N], sg[:, 0:N], ALU.mult)
        for mb in range(4):
            p0 = self.psB()
            p1 = self.psB()
            pp = (p0, p1)
            for jh in range(2):
                wd = self.w_get(("d", f, mb, jh))
                for ml in range(2):
                    for jl in range(11):
                        j = jh * 11 + jl
                        kb.mm(pp[ml][:, 0:N], wd[:, jl, ml * 128:(ml + 1) * 128], self.a[:, j, 0:N],
                              start=(j == 0), stop=(j == NJ - 1))
            for ml in range(2):
                kb.act(self.hb[:, mb * 2 + ml, 0:N], pp[ml][:, 0:N], AF.Square)
                kb.copy("act", self.f[:, mb * 2 + ml, 0:N], pp[ml][:, 0:N])
        self.dbg(f"a{f}", self.a[:, :, 0:N])
        self.dbg(f"f{f}", self.f[:, :, 0:N])
        self.residual_update(N, self.dcst, 0 if f == 1 else 8, final_tok0=(self.cur_tok0 if f == 2 else None),
                             xsrc=xsrc, mid_hook=mid_hook)
        self.dbg(f"rstdf{f}", self.rstd[:, 0:N])

    def proj_fm(self, wblk, col0, N, use_psB=False):
        kb = self.kb
        p = self.psB() if use_psB else self.psA()
        for k in range(8):
            kb.mm(p[:, 0:N], wblk[:, k, col0:col0 + 128], self.hb[:, k, 0:N], start=(k == 0), stop=(k == 7))
        return p

    def mixer(self, tok0, N, prompt):
        kb = self.kb
        L = 128 if prompt else TSP
        nchunk = N // L
        self.norm_to_hb(self.x, N, K_MPRE)
        wgr = self.wgr
        p = self.psA()
        for k in range(8):
            kb.mm(p[0:32, 0:N], wgr[:, k, 0:32], self.hb[:, k, 0:N], start=(k == 0), stop=(k == 7))
        kb.copy("act", self.gr_bf[0:32, 0:N], p[0:32, 0:N])
        import os
        ksub = int(os.environ.get("KSUB", "99"))
        if ksub < 2:
            return
        wqk = self.w_get(("in", C_GQ, 512))
        smask = self.smask_p if prompt else self.smask_s
        qv = self.qt[:, :, 0:N].rearrange("p (hp e) n -> p hp e n", e=2)
        kb.memset("pool", qv[64:128, :, 0, :], 0.0)
        kb.memset("pool", qv[0:64, :, 1, :], 0.0)
        eL = []
        for hp in range(2):
            p = self.psA()
            kb.mm(p[:, 0:N], self.wa2[0:16, hp * 128:(hp + 1) * 128], self.gr_bf[0:16, 0:N])
            e = self.tf()
            kb.act(e[:, 0:N], p[:, 0:N], AF.Exp, bias=self.dcst[:, 16 + hp:17 + hp], scale=-1.0)
            kb.act(e[:, 0:N], e[:, 0:N], AF.Ln, bias=1.0)
            la = e
            if prompt:
                kb.ts("dve", la[:, 0:N], e[:, 0:N], -1.0 / 16.0, None, ALU.mult)
            else:
                kb.tt("dve", la[:, 0:N], e[:, 0:N], self.negv_s[:, 0:N], ALU.mult)
            b = self.tf()
            kb.scan(b[:, 0:N], smask[:, 0:N], la[:, 0:N], 0.0, ALU.mult, ALU.add)
            Eb = self.tf()
            kb.act(Eb[:, 0:N], b[:, 0:N], AF.Exp)
            Enb = self.tf()
            kb.act(Enb[:, 0:N], b[:, 0:N], AF.Exp, scale=-1.0)
            pq = self.proj_fm(wqk, hp * 128, N)
            kb.stt("dve", self.qt[0:64, 2 * hp, 0:N], pq[0:64, 0:N], 0.125, Eb[0:64, 0:N], ALU.mult, ALU.mult)
            kb.stt("dve", self.qt[64:128, 2 * hp + 1, 0:N], pq[64:128, 0:N], 0.125, Eb[64:128, 0:N], ALU.mult, ALU.mult)
            pk = self.proj_fm(wqk, 256 + hp * 128, N)
            kb.tt("dve", self.kt[:, hp, 0:N], pk[:, 0:N], Enb[:, 0:N], ALU.mult)
            eLt = self.eL[hp]
            for c in range(nchunk):
                kb.copy("dve", eLt[:, c:c + 1], Eb[:, c * L + L - 1:c * L + L])
            eL.append(eLt)
        if ksub < 3:
            return
        wgv = self.w_get(("in", C_GV, 512))
        for c in range(nchunk):
            p = self.psA()
            for k in range(8):
                kb.mm(p[0:L, :], self.hb[:, k, c * L:(c + 1) * L], wgv[:, k, :], start=(k == 0), stop=(k == 7))
            dst = self.gv_tok[0:L, c, :] if prompt else self.gvs_tok[0:L, c, :]
            kb.copy("act", dst, p[0:L, :])

        wcache = {}

        def wblk(key, spec):
            if key not in wcache:
                wcache[key] = self.w_get(spec)
            return wcache[key]

        def unit_gg(h):
            w = wblk("gg", ("in", C_GG, 512))
            p = self.proj_fm(w, h * 128, N, use_psB=True)
            kb.act(self.gg_bf[:, h, 0:N], p[:, 0:N], AF.Silu)

        def unit_dq(h):
            w = wblk("dq", ("in", C_DQ, 512))
            if h == 0:
                kb.memset("pool", self.dq1[64:128, :, 0:N], 0.0)
                kb.memset("pool", self.dq2[0:64, :, 0:N], 0.0)
            p = self.proj_fm(w, h * 128, N, use_psB=True)
            kb.copy("act", self.dq1[0:64, h, 0:N], p[0:64, 0:N])
            kb.copy("act", self.dq2[64:128, h, 0:N], p[64:128, 0:N])

        def unit_dk(h):
            w = wblk("dk", ("in", C_DK, 512))
            p = self.proj_fm(w, h * 128, N, use_psB=True)
            t = self.tf()
            kb.copy("act", t[:, 0:N], p[:, 0:N])
            kb.dma("sp", self.kT_out[h * 128:(h + 1) * 128, tok0:tok0 + N], t[:, 0:N], is_output=True)
            if prompt:
                kb.copy("dve", self.KT[:, h, tok0:tok0 + N], p[:, 0:N])
            else:
                kb.copy("dve", self.KT_s[:, h, 0:N], p[:, 0:N])

        def unit_dv(c):
            w = wblk("dv", ("in", C_DV, 512))
            p = self.psB()
            for k in range(8):
                kb.mm(p[0:L, :], self.hb[:, k, c * L:(c + 1) * L], w[:, k, :], start=(k == 0), stop=(k == 7))
            t = self.vtok_f[self.vtf_i % 1]
            self.vtf_i += 1
            kb.copy("act", t[0:L, :], p[0:L, :])
            kb.dma("sp", self.v_out[tok0 + c * L:tok0 + (c + 1) * L, :], t[0:L, :], is_output=True)
            if prompt:
                kb.copy("dve", self.VS[:, (tok0 // 128) + c, :], p[:, :])
            else:
                kb.copy("dve", self.dvs_tok[0:L, c, :], p[0:L, :])

        units = [(unit_gg, i) for i in range(4)] + [(unit_dq, i) for i in range(4)] + \
                [(unit_dk, i) for i in range(4)] + [(unit_dv, i) for i in range(nchunk)]
        if ksub < 7:
            return
        if self.stage < 4:
            kb.memset("dve", self.merge[:, 4:8, :], 0.0)
        self.gla(tok0, N, L, nchunk, prompt, eL, units)
        if self.stage >= 4:
            if prompt:
                last_finish = self.attn_prompt(tok0, N)
            else:
                kb.memset("pool", self.merge[:, 4:8, 0:N], 0.0)
                last_finish = self.attn_sample(N)
        else:
            last_finish = (lambda: None)
        wo = self.w_get(("out", 0))
        banks = [self.pb[0], self.pb[1], self.pb[2], self.pb[3]]
        for k in range(7):
            for ml in range(4):
                kb.mm(banks[ml][:, 0:N], wo[:, k, ml * 128:(ml + 1) * 128], self.merge[:, k, 0:N], start=(k == 0), stop=False)
        last_finish()
        for ml in range(4):
            kb.mm(banks[ml][:, 0:N], wo[:, 7, ml * 128:(ml + 1) * 128], self.merge[:, 7, 0:N], start=False, stop=True)
            kb.act(self.hb[:, ml, 0:N], banks[ml][:, 0:N], AF.Square)
            kb.copy("act", self.f[:, ml, 0:N], banks[ml][:, 0:N])
        wo = self.w_get(("out", 1))
        for ml in range(4):
            p = self.psB()
            for k in range(8):
                kb.mm(p[:, 0:N], wo[:, k, ml * 128:(ml + 1) * 128], self.merge[:, k, 0:N], start=(k == 0), stop=(k == 7))
            kb.act(self.hb[:, 4 + ml, 0:N], p[:, 0:N], AF.Square)
            kb.copy("act", self.f[:, 4 + ml, 0:N], p[:, 0:N])
        self.residual_update(N, self.cst, K_MPOST)

    def gla(self, tok0, N, L, nchunk, prompt, eL, units=()):
        kb = self.kb
        units = list(units)
        gv = self.gv_tok if prompt else self.gvs_tok

        def stage1(c):
            cols = slice(c * L, (c + 1) * L)
            pt = self.psA()
            ptb = pt[:].bitcast(BF16)
            for hp in range(2):
                kb.transpose(ptb[0:L, hp * 128:(hp + 1) * 128], self.kt[:, hp, cols], self.ident[:, :])
            ktk = self.tb()
            kb.copy("dve", ktk[0:L, 0:256], ptb[0:L, 0:256])
            pab = (self.psA(), self.psA())
            for h in range(4):
                hp = h // 2
                rows = slice((h % 2) * 64, (h % 2) * 64 + 64)
                kb.mm(pab[h % 2][0:L, h * 128:h * 128 + L], self.kt[:, hp, cols], self.qt[:, h, cols])
            at = self.tb()
            for h in range(4):
                kb.tt("dve", at[0:L, h * 128:h * 128 + L], pab[h % 2][0:L, h * 128:h * 128 + L], self.cmask[0:L, 0:L], ALU.mult)
            return at, ktk

        def stage2(c, at, ktk):
            cols = slice(c * L, (c + 1) * L)
            if not prompt:
                kb.dma("sp", self.S[:], self.state_s[c])
                kb.copy("dve", self.S_bf[:], self.S[:])
            po = self.psA()
            pS = self.psA()
            for h in range(4):
                hp = h // 2
                rows = slice((h % 2) * 64, (h % 2) * 64 + 64)
                kb.mm(po[:, h * 128:h * 128 + L], gv[0:L, c, h * 128:(h + 1) * 128], at[0:L, h * 128:h * 128 + L],
                      start=True, stop=False)
                kb.mm(po[:, h * 128:h * 128 + L], self.S_bf[:, hp, :], self.qt[:, h, cols], start=False, stop=True)
            for h in range(4):
                hp = h // 2
                kb.mm(pS[:, h * 128:(h + 1) * 128], ktk[0:L, hp * 128:(hp + 1) * 128], gv[0:L, c, h * 128:(h + 1) * 128])
            for h in range(4):
                kb.copy("act", self.o_sb[:, h, cols], po[:, h * 128:h * 128 + L])
                kb.act(self.f[:, h, :].bitcast(BF16)[:, cols], po[:, h * 128:h * 128 + L], AF.Square)
            t = self.tf()
            pSv = pS[:, 0:512].rearrange("p (hp e v) -> p hp e v", hp=2, e=2)
            tv = t[:, 0:256].rearrange("p (hp v) -> p hp v", hp=2)
            kb.tt("dve", tv[0:64, :, :], pSv[0:64, :, 0, :], self.S[0:64, :, :], ALU.add)
            kb.tt("dve", tv[64:128, :, :], pSv[64:128, :, 1, :], self.S[64:128, :, :], ALU.add)
            for hp in range(2):
                kb.ts("dve", self.S[:, hp, :], tv[:, hp, :], eL[hp][:, c:c + 1], None, ALU.mult)
            kb.copy("dve", self.S_bf[:], self.S[:])
            if not prompt:
                kb.dma("sp", self.S_s[c], self.S[:], is_output=True)

        cur = stage1(0)
        for c in range(nchunk):
            nxt = stage1(c + 1) if c + 1 < nchunk else None
            stage2(c, *cur)
            for _ in range(4):
                if units:
                    fn, arg = units.pop(0)
                    fn(arg)
            cur = nxt
        while units:
            fn, arg = units.pop(0)
            fn(arg)
        pss = [self.psB() for h in range(4)]
        rts = [self.tf() for h in range(4)]
        for h in range(4):
            kb.mm(pss[h][:, 0:N], self.ones_bf[:, :], self.f[:, h, :].bitcast(BF16)[:, 0:N])
        for h in range(4):
            kb.act(rts[h][:, 0:N], pss[h][:, 0:N], AF.Ln, bias=EPS, scale=1.0 / 128)
        for h in range(4):
            kb.act(rts[h][:, 0:N], rts[h][:, 0:N], AF.Exp, scale=-0.5)
        for h in range(4):
            oh = self.o_sb[:, h, 0:N]
            kb.stt("dve", oh, oh, self.cst[:, K_GLAG:K_GLAG + 1], rts[h][:, 0:N], ALU.mult, ALU.mult)
            kb.tt("dve", self.merge[:, h, 0:N], oh, self.gg_bf[:, h, 0:N], ALU.mult)

    def attn_core(self, N, h, q_aps, ktiles, hook=None, G=1):
        kb = self.kb
        O = (self.pb[4], self.pb[5])
        Ls = (self.pb[6], self.pb[7])
        nt = len(ktiles)
        groups = []
        j = 0
        while j < nt:
            if G > 1 and ktiles[j][2] == 128 and ktiles[j][3] == 128 and not ktiles[j][5]:
                g = [j]
                while len(g) < G and g[-1] + 1 < nt and ktiles[g[-1] + 1][2] == 128 and ktiles[g[-1] + 1][3] == 128 \
                        and not ktiles[g[-1] + 1][5]:
                    g.append(g[-1] + 1)
                groups.append(g)
                j = g[-1] + 1
            else:
                groups.append([j])
                j += 1

        def qk(grp):
            pss = []
            for c in range(2):
                ps = self.psA()
                for gi, j in enumerate(grp):
                    KTa, Va, nkM, nk, q0, diag = ktiles[j]
                    if len(grp) == 1:
                        kb.mm(ps[0:nkM, q0:N], KTa, q_aps[c][:, q0:N])
                    else:
                        kb.mm(ps[0:nkM, gi * N:(gi + 1) * N], KTa, q_aps[c][:, 0:N])
                pss.append(ps)
            return pss

        def rest(grp, pss):
            for c in range(2):
                ps = pss[c]
                P = self.tb()
                if len(grp) == 1:
                    j = grp[0]
                    KTa, Va, nkM, nk, q0, diag = ktiles[j]
                    kb.act(P[0:nk, q0:N], ps[0:nk, q0:N], AF.Exp, scale=0.125)
                    if diag:
                        kb.memset("dve", P[64:128, q0:q0 + 64], 0.0)
                    kb.mm(O[c][:, q0:N], Va, P[0:nk, q0:N], start=(j == 0), stop=(j == nt - 1))
                    kb.mm(Ls[c][:, q0:N], self.ones_bf[0:nk, :], P[0:nk, q0:N], start=(j == 0), stop=(j == nt - 1))
                else:
                    W = len(grp) * N
                    kb.act(P[:, 0:W], ps[:, 0:W], AF.Exp, scale=0.125)
                    for gi, j in enumerate(grp):
                        Va = ktiles[j][1]
                        kb.mm(O[c][:, 0:N], Va, P[:, gi * N:(gi + 1) * N], start=(j == 0), stop=(j == nt - 1))
                    for gi, j in enumerate(grp):
                        kb.mm(Ls[c][:, 0:N], self.ones_bf[:, :], P[:, gi * N:(gi + 1) * N], start=(j == 0), stop=(j == nt - 1))

        prev = None
        for gidx, grp in enumerate(groups):
            pss = qk(grp)
            if prev is not None:
                rest(*prev)
            prev = (grp, pss)
            if hook is not None and gidx == min(2, len(groups) - 1):
                hook()
                hook = None
        rest(*prev)
        if hook is not None:
            hook()
        r1 = self.tf()
        r2 = self.tf()
        t1 = self.tf()
        t2 = self.tf()
        kb.recip(r1[:, 0:N], Ls[0][:, 0:N])
        kb.copy("act", t1[:, 0:N], O[0][:, 0:N])
        kb.recip(r2[:, 0:N], Ls[1][:, 0:N])
        kb.copy("act", t2[:, 0:N], O[1][:, 0:N])
        return (r1, r2, t1, t2)

    def attn_finishA(self, ev, N, h):
        kb = self.kb
        r1, r2, t1, t2 = ev
        kb.tt("dve", t1[:, 0:N], t1[:, 0:N], r1[:, 0:N], ALU.mult)
        kb.tt("dve", t2[:, 0:N], t2[:, 0:N], r2[:, 0:N], ALU.mult)
        od = self.o_sb[:, h, 0:N]
        kb.stt("dve", od, t2[:, 0:N], self.neg_lam, t1[:, 0:N], ALU.mult, ALU.add)
        kb.act(self.hb[:, 0, 0:N], od, AF.Square)

    def attn_finishB(self, N, h, cols=None, bank=None):
        kb = self.kb
        od = self.o_sb[:, h, 0:N]
        if bank is not None:
            p = bank
        else:
            p = self.psA()
            if self.pa_i % 2 == 1:
                self.pa_i += 1
        kb.mm(p[:, 0:N], self.ones_bf[:, :], self.hb[:, 0, 0:N])
        kb.act(self.rstd[:, 0:N], p[:, 0:N], AF.Ln, bias=EPS, scale=1.0 / 128)
        kb.act(self.rstd[:, 0:N], self.rstd[:, 0:N], AF.Exp, scale=-0.5)
        dst = self.merge[:, 4 + h, 0:N] if cols is None else self.merge[:, 4 + h, cols]
        kb.stt("dve", dst, od, self.dg08, self.rstd[:, 0:N], ALU.mult, ALU.mult)

    def attn_prompt(self, tok0, N):
        i = tok0 // NTK
        pending = None
        for h in range(4):
            ktiles = []
            for j in range(4 * i + 4):
                m = j - 4 * i
                q0 = 0 if m < 0 else 128 * m
                KTa = self.KT[:, h, j * 128:(j + 1) * 128]
                Va = self.VS[:, j, h * 128:(h + 1) * 128]
                ktiles.append((KTa, Va, 128, 128, q0, m >= 0))
            hook = None
            if pending is not None:
                pe, ph = pending
                hook = (lambda pe=pe, ph=ph: self.attn_finishA(pe, N, ph))
            ev = self.attn_core(N, h, (self.dq1[:, h, 0:N], self.dq2[:, h, 0:N]), ktiles, hook)
            if pending is not None:
                self.attn_finishB(N, pending[1])
            pending = (ev, h)
        return (lambda: (self.attn_finishA(pending[0], N, pending[1]), self.attn_finishB(N, pending[1], bank=self.pb[4])))

    def cache_prefetch(self, upto):
        while self.cache_loaded <= min(upto, 4 * NS - 1):
            i = self.cache_loaded
            h, s_ = i // NS, i % NS
            self.kb.dma("pool", self.kc_sb[i % self.NCACHE][:], self.kTc[s_, h])
            self.kb.dma("pool", self.vc_sb[i % self.NCACHE][:], self.vc[s_, :, h, :].rearrange("(j p) v -> p j v", p=128))
            self.cache_loaded += 1

    def attn_sample(self, N):
        kb = self.kb
        pending = None
        for h in range(4):
            for s in range(NS):
                i = self.cache_i
                self.cache_i += 1
                kc = self.kc_sb[i % self.NCACHE]
                vcs = self.vc_sb[i % self.NCACHE]
                self.cache_prefetch(i + self.NCACHE - 1)
                cols = slice(s * TSP, s * TSP + TS)
                colsM = slice(s * TSP, (s + 1) * TSP)
                ktiles = []
                for j in range(PAST // 128):
                    ktiles.append((kc[:, j * 128:(j + 1) * 128], vcs[:, j, :], 128, 128, 0, False))
                ktiles.append((self.KT_s[:, h, colsM], self.dvs_tok[0:TS, s, h * 128:(h + 1) * 128], TSP, TS, 0, False))
                hook = None
                if pending is not None:
                    pe, ph, pc = pending
                    hook = (lambda pe=pe, ph=ph: self.attn_finishA(pe, TS, ph))
                ev = self.attn_core(TS, h, (self.dq1[:, h, cols], self.dq2[:, h, cols]), ktiles, hook, G=8)
                if pending is not None:
                    self.attn_finishB(TS, pending[1], pending[2])
                pending = (ev, h, cols)
        return (lambda: (self.attn_finishA(pending[0], TS, pending[1]), self.attn_finishB(TS, pending[1], pending[2], bank=self.pb[4])))

    def load_x(self, tok0, N):
        for c in range(8):
            self.kb.dma("pool", self.x2[:, c, 0:N], self.xT[c * 128:(c + 1) * 128, tok0:tok0 + N])

    def tile(self, tok0, N, prompt, first=False, last=False, nxt=None):
        kb = self.kb
        self.cur_tok0 = tok0
        self.stage = 9
        if not prompt:
            self.cache_prefetch(self.NCACHE - 2)
        self.ffn(1, N, xsrc=self.x2)
        self.mixer(tok0, N, prompt)
        if nxt is not None:
            self.load_x(nxt[0], nxt[1])
        self.ffn_prenorm(2, N, self.x)
        hook = None
        if nxt is not None:
            hook = (lambda: self.ffn_prenorm(1, nxt[1], self.x2))
        self.ffn(2, N, xsrc=self.x, mid_hook=hook)
        if prompt and last:
            kb.dma("sp", self.S_p, self.S[:], is_output=True)


_PROG_CACHE = {}


def _get_prog(n_tiles=8, do_sample=True):
    key = (n_tiles, do_sample)
    if key not in _PROG_CACHE:
        nc = bass.Bass("TRN2", target_bir_lowering=False)
        Prog(nc, n_tiles=n_tiles, do_sample=do_sample)
        _PROG_CACHE[key] = nc
    return _PROG_CACHE[key]


def _lay_wgr(w_in):
    g = np.zeros((128, 8, 32), np.float32)
    g[:, :, 0:16] = np.asarray(w_in[:, C_GR:C_GR + 16], np.float32).reshape(8, 128, 16).transpose(1, 0, 2)
    return np.ascontiguousarray(g.reshape(128, 256))


def _pack_consts(inp):
    c = np.zeros((128, NCST), np.float32)
    def g8(v):
        return np.ascontiguousarray(np.asarray(v, np.float32).reshape(8, 128).T)
    c[:, K_F1PRE:K_F1PRE + 8] = g8(inp["ffn1_pre_g"][0])
    c[:, K_F1POST:K_F1POST + 8] = g8(inp["ffn1_post_g"][0])
    c[:, K_MPRE:K_MPRE + 8] = g8(inp["mix_pre_g"][0])
    c[:, K_MPOST:K_MPOST + 8] = g8(inp["mix_post_g"][0])
    c[:, K_F2PRE:K_F2PRE + 8] = g8(inp["ffn2_pre_g"][0])
    c[:, K_F2POST:K_F2POST + 8] = g8(inp["ffn2_post_g"][0])
    c[:, K_BA:K_BA + 2] = np.asarray(inp["b_gate_a"][0], np.float32).reshape(2, 128).T
    c[:, K_GLAG] = inp["gla_norm_g"][0]
    c[:, K_DIFFG] = inp["diff_norm_g"][0]
    for i, k in enumerate(("lambda_q1", "lambda_k1", "lambda_q2", "lambda_k2")):
        c[0:64, K_LAM + i] = inp[k][0]
    return c


def _run(inp, n_tiles=8, do_sample=True):
    import os
    do_sample = do_sample and os.environ.get("KNOSAMPLE", "0") != "1"
    nc = _get_prog(n_tiles, do_sample)
    T = n_tiles * NTK
    f32 = lambda a: np.ascontiguousarray(np.asarray(a, np.float32))
    cst = _pack_consts(inp)
    ident = np.eye(128, dtype=np.float32)
    cmask = np.triu(np.ones((128, 128), np.float32))
    shared = {
        "cst": cst, "w_in": f32(inp["w_in"][0]), "w_a2": f32(inp["w_gate_a2"][0]), "w_out": f32(inp["w_out"][0]),
        "f1g": f32(inp["ffn1_w_gate"][0]), "f1u": f32(inp["ffn1_w_up"][0]), "f1d": f32(inp["ffn1_w_down"][0]),
        "f2g": f32(inp["ffn2_w_gate"][0]), "f2u": f32(inp["ffn2_w_up"][0]), "f2d": f32(inp["ffn2_w_down"][0]),
        "ident": ident, "cmask": cmask,
        "w_gr": _lay_wgr(inp["w_in"][0]),
    }
    in_maps = []
    for c in range(8):
        xp = np.asarray(inp["x_prompt"][c][:T], np.float32).T
        xs = np.zeros((NS, TSP, D), np.float32)
        xs[:, :TS] = np.asarray(inp["x_sample"][NS * c:NS * c + NS], np.float32)
        xs = xs.reshape(NS * TSP, D).T
        xT = np.ascontiguousarray(np.concatenate([xp, xs], axis=1)) if do_sample else np.ascontiguousarray(xp)
        if not do_sample:
            m = dict(shared)
            m["xT"] = xT
            in_maps.append(m)
            continue
        st = np.asarray(inp["state_gla"][0, NS * c:NS * c + NS], np.float32)
        st = st.reshape(NS, 2, 2, 64, 128).transpose(0, 2, 3, 1, 4).reshape(NS, 128, 2, 128)
        kTc = np.asarray(inp["cache_diff_k"][0, NS * c:NS * c + NS], np.float32).transpose(0, 2, 3, 1)
        vc = np.asarray(inp["cache_diff_v"][0, NS * c:NS * c + NS], np.float32)
        m = dict(shared)
        m.update({"xT": xT, "state_s": f32(st), "kTc": f32(kTc), "vc": f32(vc)})
        in_maps.append(m)
    res = run_bass_kernel_spmd(nc, in_maps, core_ids=list(range(8)))
    if not do_sample:
        for r in res.results:
            r["yT"] = np.concatenate([r["yT"], np.zeros((D, NS * TSP), np.float32)], axis=1)
            r["kT_out"] = np.concatenate([r["kT_out"], np.zeros((512, NS * TSP), np.float32)], axis=1)
            r["v_out"] = np.concatenate([r["v_out"], np.zeros((NS * TSP, 512), np.float32)], axis=0)
            r["S_s"] = np.zeros((NS, 128, 2, 128), np.float32)
    return res.results, T


def _assemble(results, T):
    y_p = np.zeros((8, T, D), np.float32)
    y_s = np.zeros((8 * NS, TS, D), np.float32)
    sg_p = np.zeros((1, 8, 4, 64, 128), np.float32)
    k_p = np.zeros((1, 8, T, 4, 128), np.float32)
    v_p = np.zeros((1, 8, T, 4, 128), np.float32)
    sg_s = np.zeros((1, 8 * NS, 4, 64, 128), np.float32)
    k_s = np.zeros((1, 8 * NS, TS, 4, 128), np.float32)
    v_s = np.zeros((1, 8 * NS, TS, 4, 128), np.float32)
    unS = lambda s: s.reshape(2, 64, 2, 128).transpose(2, 0, 1, 3).reshape(4, 64, 128)
    for c in range(8):
        r = results[c]
        yT = r["yT"]
        y_p[c] = yT[:, :T].T
        y_s[NS * c:NS * c + NS] = yT[:, T:].T.reshape(NS, TSP, D)[:, :TS]
        kT = r["kT_out"]
        k_p[0, c] = kT[:, :T].T.reshape(T, 4, 128)
        k_s[0, NS * c:NS * c + NS] = kT[:, T:].T.reshape(NS, TSP, 4, 128)[:, :TS]
        vo = r["v_out"]
        v_p[0, c] = vo[:T].reshape(T, 4, 128)
        v_s[0, NS * c:NS * c + NS] = vo[T:].reshape(NS, TSP, 4, 128)[:, :TS]
        sg_p[0, c] = unS(r["S_p"])
        for s in range(NS):
            sg_s[0, NS * c + s] = unS(r["S_s"][s])
    return (y_p, y_s, sg_p, k_p, v_p, sg_s, k_s, v_s)


def kernel(**inputs):
    results, T = _run(inputs, 8, True)
    return _assemble(results, T)
```

```python
import concourse.bass as bass
import concourse.mybir as mybir

F32 = mybir.dt.float32
BF16 = mybir.dt.bfloat16
ALU = mybir.AluOpType
AF = mybir.ActivationFunctionType
PG = 64

_DSZ = {F32: 4, BF16: 2, mybir.dt.int32: 4, mybir.dt.uint32: 4, mybir.dt.float16: 2,
        mybir.dt.uint16: 2, mybir.dt.int16: 2, mybir.dt.uint8: 1, mybir.dt.int8: 1}


class Op:
    __slots__ = ("eng", "fn", "deps", "idx", "tick", "sem", "semval", "is_dma", "need_inc", "name")

    def __init__(self, eng, fn, is_dma=False, name=""):
        self.eng = eng
        self.fn = fn
        self.deps = []
        self.is_dma = is_dma
        self.need_inc = False
        self.tick = None
        self.sem = None
        self.semval = None
        self.name = name


class Res:
    __slots__ = ("w", "r")

    def __init__(self):
        self.w = None
        self.r = []


class KB:
    ENGS = ("pe", "act", "dve", "pool", "sp")

    def __init__(self, nc, n_dma_sems=(24, 40, 2)):
        self.nc = nc
        self.ops = {e: [] for e in self.ENGS}
        self.res = {}
        self.tinfo = {}
        self.sb_off = 16640
        self.ps_off = 0
        self.n_dma_sems = dict(sp=n_dma_sems[0], pool=n_dma_sems[1], act=n_dma_sems[2])
        self.dma_hist = {"sp": [], "pool": [], "act": []}
        self.out_dmas = []
        self.nops = 0

    def sb(self, name, shape, dtype, at=None):
        nbytes = _DSZ[dtype]
        for s in shape[1:]:
            nbytes *= s
        if at is None:
            at = (self.sb_off + 63) // 64 * 64
            self.sb_off = at + nbytes
        t = self.nc.alloc_sbuf_tensor_at(name, list(shape), dtype, offset=at)
        self.tinfo[t.name] = ("S", at, nbytes)
        return t

    def sb_mark(self):
        return self.sb_off

    def sb_reset(self, mark):
        self.sb_off = mark

    def ps(self, name, shape, dtype=F32):
        nbytes = _DSZ[dtype]
        for s in shape[1:]:
            nbytes *= s
        t = self.nc.alloc_psum_tensor(name, list(shape), dtype)
        nb = (nbytes + 2047) // 2048
        self.tinfo[t.name] = ("P", self.ps_off, nb * 2048)
        self.ps_off += nb * 2048
        return t

    def dram_token(self, name):
        return ("D", name)

    def pages(self, ap):
        if isinstance(ap, tuple):
            return [ap]
        name = ap.tensor.name
        info = self.tinfo.get(name)
        if info is None:
            return []
        space, base, bpp = info
        dsz = _DSZ[ap.dtype]
        pstep = bpp // dsz
        apl = ap.ap
        fo = ap.offset % pstep
        ext = 0
        for (s, c) in apl[1:]:
            if s > 0:
                ext += (c - 1) * s
        lo = base + fo * dsz
        hi = base + (fo + ext + 1) * dsz - 1
        return [(space, p) for p in range(lo // PG, hi // PG + 1)]

    def _add(self, op, reads, writes):
        rp = set()
        for a in reads:
            rp.update(self.pages(a))
        wp = set()
        for a in writes:
            wp.update(self.pages(a))
        for p in list(rp) + list(wp):
            if p[0] == "P":
                wp.add(("PB", p[1] // (2048 // PG)))
        deps = set()
        for p in rp:
            r = self.res.get(p)
            if r is None:
                r = self.res[p] = Res()
            w = r.w
            if w is not None:
                if w.eng == op.eng and not w.is_dma and not op.is_dma and op.eng == "pe":
                    pass
                else:
                    deps.add(w)
        for p in wp:
            r = self.res.get(p)
            if r is None:
                r = self.res[p] = Res()
            w = r.w
            if w is not None:
                if w.eng == op.eng and not w.is_dma and not op.is_dma and op.eng == "pe":
                    pass
                else:
                    deps.add(w)
            for rd in r.r:
                if rd is op:
                    continue
                if rd.eng == op.eng and not rd.is_dma and not op.is_dma and op.eng == "pe":
                    continue
                deps.add(rd)
        for p in rp:
            r = self.res[p]
            if op.is_dma:
                r.r.append(op)
            else:
                r.r = [x for x in r.r if x.is_dma or x.eng != op.eng]
                r.r.append(op)
        for p in wp:
            r = self.res[p]
            r.w = op
            r.r = []
        deps.discard(op)
        op.deps = list(deps)
        op.idx = self.nops
        self.nops += 1
        self.ops[op.eng].append(op)
        return op

    def op(self, eng, fn, reads=(), writes=(), name=""):
        return self._add(Op(eng, fn, False, name), reads, writes)

    def dma(self, q, out, in_, extra_reads=(), extra_writes=(), is_output=False, name=""):
        o = Op(q, lambda e, out=out, in_=in_: e.dma_start(out=out, in_=in_), True, name)
        hist = self.dma_hist[q]
        n = self.n_dma_sems[q]
        k = len(hist)
        o.sem = (q, k % n)
        o.semval = 16 * (k // n + 1)
        self._add(o, [in_] + list(extra_reads), [out] + list(extra_writes))
        if k >= n:
            prev = hist[k - n]
            if prev not in o.deps:
                o.deps.append(prev)
        hist.append(o)
        if is_output:
            self.out_dmas.append(o)
        return o

    def mm(self, out, lhsT, rhs, start=True, stop=True, **kw):
        return self.op("pe", lambda e: e.matmul(out, lhsT, rhs, start=start, stop=stop, **kw),
                       [lhsT, rhs] + ([] if start else []), [out])

    def transpose(self, out, in_, ident):
        return self.op("pe", lambda e: e.transpose(out, in_, ident), [in_, ident], [out])

    def act(self, out, in_, func, bias=None, scale=1.0, accum_out=None, eng="act"):
        reads = [in_]
        kw = {}
        if bias is not None:
            kw["bias"] = bias
            if not isinstance(bias, (int, float)):
                reads.append(bias)
        if not isinstance(scale, (int, float)):
            reads.append(scale)
        writes = [out]
        if accum_out is not None:
            kw["accum_out"] = accum_out
            writes.append(accum_out)
        return self.op("act", lambda e: e.activation(out, in_, func, scale=scale, **kw), reads, writes)

    def tt(self, eng, out, in0, in1, op):
        return self.op(eng, lambda e: e.tensor_tensor(out, in0, in1, op), [in0, in1], [out])

    def ts(self, eng, out, in0, s1, s2, op0, op1=None):
        reads = [in0]
        if not isinstance(s1, (int, float)):
            reads.append(s1)
        if s2 is not None and not isinstance(s2, (int, float)):
            reads.append(s2)
        if op1 is None:
            s2 = 0.0
            op1 = ALU.add
        return self.op(eng, lambda e: e.tensor_scalar(out, in0, s1, s2, op0, op1), reads, [out])

    def stt(self, eng, out, in0, scalar, in1, op0, op1):
        reads = [in0, in1]
        if not isinstance(scalar, (int, float)):
            reads.append(scalar)
        return self.op(eng, lambda e: e.scalar_tensor_tensor(out, in0, scalar, in1, op0, op1), reads, [out])

    def copy(self, eng, out, in_):
        if eng == "act":
            return self.op("act", lambda e: e.copy(out, in_), [in_], [out])
        return self.op(eng, lambda e: e.tensor_copy(out, in_), [in_], [out])

    def memset(self, eng, out, val):
        return self.op(eng, lambda e: e.memset(out, val), [], [out])

    def recip(self, out, in_):
        return self.op("dve", lambda e: e.reciprocal(out, in_), [in_], [out])

    def scan(self, out, d0, d1, init, op0, op1, eng="dve"):
        reads = [d0, d1]
        if not isinstance(init, (int, float)):
            reads.append(init)
        return self.op(eng, lambda e: e.tensor_tensor_scan(out, d0, d1, init, op0, op1), reads, [out])

    def emit(self):
        nc = self.nc
        for e in self.ENGS:
            for o in self.ops[e]:
                for d in o.deps:
                    d.need_inc = True
        for o in self.out_dmas:
            o.need_inc = True
        for q in self.dma_hist:
            for o in self.dma_hist[q]:
                o.need_inc = True
        for e in ("pe", "act", "dve", "pool"):
            t = 0
            for o in self.ops[e]:
                if o.is_dma:
                    continue
                if o.need_inc:
                    t += 1
                    o.tick = t
        sems = {}
        import contextlib
        with contextlib.ExitStack() as st:
            for e in ("pe", "act", "dve", "pool"):
                sems[e] = st.enter_context(nc.semaphore("c_" + e))
            for q in ("sp", "pool", "act"):
                for i in range(self.n_dma_sems[q]):
                    sems[(q, i)] = st.enter_context(nc.semaphore(f"d_{q}{i}"))
            block = st.enter_context(nc.Block())

            def gen(ename):
                def body(eng):
                    waited = {}
                    for o in self.ops[ename]:
                        need = {}
                        for d in o.deps:
                            if d.is_dma:
                                key, val = d.sem, d.semval
                            else:
                                key, val = d.eng, d.tick
                            if waited.get(key, 0) >= val:
                                continue
                            if need.get(key, 0) < val:
                                need[key] = val
                        for key, val in need.items():
                            eng.wait_ge(sems[key], val)
                            waited[key] = val
                        ins = o.fn(eng)
                        if o.is_dma:
                            ins.then_inc(sems[o.sem], 16)
                        elif o.need_inc:
                            ins.then_inc(sems[o.eng], 1)
                    if ename == "sp":
                        need = {}
                        alld = self.out_dmas + [d for q in self.dma_hist for d in self.dma_hist[q]]
                        for d in alld:
                            if need.get(d.sem, 0) < d.semval:
                                need[d.sem] = d.semval
                        for key, val in need.items():
                            if waited.get(key, 0) < val:
                                eng.wait_ge(sems[key], val)
                return body

            block.tensor(gen("pe"))
            block.scalar(gen("act"))
            block.vector(gen("dve"))
            block.gpsimd(gen("pool"))
            block.sync(gen("sp"))

import numpy as np
from concourse.bass_utils import run_bass_kernel_spmd

D = 1024
DFF = 2816
NJ = 22
T_FULL = 4096
NTK = 512
NS = 4
TS = 16
TSP = 32
PAST = 2048
EPS = 1e-6
LAM_INIT = 0.2
C_GQ, C_GK, C_GV, C_GR, C_GG, C_DQ, C_DK, C_DV = 0, 256, 512, 1024, 1040, 1552, 2064, 2576
K_F1PRE, K_F1POST, K_MPRE, K_MPOST, K_F2PRE, K_F2POST, K_BA, K_GLAG, K_DIFFG, K_LAM = 0, 8, 16, 24, 32, 40, 48, 50, 51, 52
NCST = 56


class Prog:
    def __init__(self, nc, n_tiles=8, do_sample=True):
        self.nc = nc
        self.kb = KB(nc)
        self.dbg_names = set()
        self.n_tiles = n_tiles
        self.do_sample = do_sample
        self.T = n_tiles * NTK
        self.build()

    def dram_in(self, name, shape):
        return self.nc.dram_tensor(name, list(shape), F32, kind="ExternalInput").ap()

    def dram_out(self, name, shape):
        return self.nc.dram_tensor(name, list(shape), F32, kind="ExternalOutput").ap()

    def build(self):
        kb = self.kb
        T = self.T
        TT = T + (NS * TSP if self.do_sample else 0)
        self.xT = self.dram_in("xT", [D, TT])
        self.cst_d = self.dram_in("cst", [128, NCST])
        self.w_in = self.dram_in("w_in", [D, 3088])
        self.w_a2 = self.dram_in("w_a2", [16, 256])
        self.w_gr = self.dram_in("w_gr", [128, 256])
        self.w_out = self.dram_in("w_out", [D, D])
        self.fw = {}
        for f in (1, 2):
            self.fw[f] = (self.dram_in(f"f{f}g", [D, DFF]), self.dram_in(f"f{f}u", [D, DFF]),
                          self.dram_in(f"f{f}d", [DFF, D]))
        if self.do_sample:
            self.state_s = self.dram_in("state_s", [NS, 128, 2, 128])
            self.kTc = self.dram_in("kTc", [NS, 4, 128, PAST])
            self.vc = self.dram_in("vc", [NS, PAST, 4, 128])
        self.ident_d = self.dram_in("ident", [128, 128])
        self.cmask_d = self.dram_in("cmask", [128, 128])
        self.yT = self.dram_out("yT", [D, TT])
        self.kT_out = self.dram_out("kT_out", [512, TT])
        self.v_out = self.dram_out("v_out", [TT, 512])
        self.S_p = self.dram_out("S_p", [128, 2, 128])
        if self.do_sample:
            self.S_s = self.dram_out("S_s", [NS, 128, 2, 128])

        sb = kb.sb
        self.cst = sb("cst_sb", [128, NCST], F32)
        self.dcst = sb("dcst", [128, 32], F32)
        self.ones_bf = sb("ones_bf", [128, 128], BF16)
        self.ident = sb("ident_sb", [128, 128], BF16)
        self.cmask = sb("cmask_sb", [128, 128], BF16)
        self.smask_p = sb("smask_p", [128, NTK], F32)
        self.smask_s = sb("smask_s", [128, NS * TSP], F32)
        self.negv_s = sb("negv_s", [128, NS * TSP], F32)
        self.wa2 = sb("wa2_sb", [16, 256], BF16)
        self.wgr_f = sb("wgr_f", [128, 256], F32)
        self.ones_f = self.wgr_f[:, 0:128]
        self.wgr = sb("wgr", [128, 8, 32], BF16)
        self.KT = sb("KT", [128, 4, T_FULL], BF16)
        self.VS = sb("VS", [128, 32, 512], BF16)
        self.x = sb("x", [128, 8, NTK], F32)
        self.hb = sb("hb", [128, 8, NTK], BF16)
        self.f = sb("f", [128, 8, NTK], F32)
        self.rstd = sb("rstd", [128, NTK], F32)
        self.NSLOT = 5
        self.wring = [sb(f"wr{i}", [128, 4096], BF16) for i in range(self.NSLOT)]
        um = kb.sb_mark()
        self.a = sb("a", [128, NJ, NTK], BF16)
        uend = kb.sb_mark()
        kb.sb_reset(um)
        self.gg_bf = sb("gg_bf", [128, 4, NTK], BF16)
        self.dq1 = sb("dq1", [128, 4, NTK], BF16)
        self.dq2 = sb("dq2", [128, 4, NTK], BF16)
        self.gv_tok = sb("gv_tok", [128, 4, 512], BF16)
        self.qt = sb("qt", [128, 4, NTK], BF16)
        self.kt = sb("kt", [128, 2, NTK], BF16)
        self.o_sb = sb("o_sb", [128, 4, NTK], F32)
        self.merge = sb("merge", [128, 8, NTK], BF16)
        kb.sb_reset(max(uend, kb.sb_mark()))
        self.x2 = sb("x2", [128, 8, NTK], F32, at=kb.tinfo[self.o_sb.name][1])
        assert kb.tinfo[self.merge.name][1] == kb.tinfo[self.o_sb.name][1] + 8192
        self.tfp = [sb(f"tf{i}", [128, NTK], F32) for i in range(4)]
        self.tbp = [sb(f"tb{i}", [128, NTK], BF16) for i in range(4)]
        self.tfi = 0
        self.tbi = 0
        self.S = sb("S", [128, 2, 128], F32)
        self.S_bf = sb("S_bf", [128, 2, 128], BF16)
        self.ktok = sb("ktok", [128, 2, 128], BF16)
        self.at_sb = sb("at_sb", [128, 128], BF16)
        self.gr_bf = sb("gr_bf", [32, NTK], BF16)
        self.eL = [sb(f"eL{i}", [128, 8], F32) for i in range(2)]
        self.cache_i = 0
        self.vtok_f = [sb(f"vtokf{i}", [128, 512], F32) for i in range(1)]
        self.vtf_i = 0
        print('SBUF used', kb.sb_off)
        assert kb.sb_off <= 229312, kb.sb_off
        ktb = kb.tinfo[self.KT.name][1]
        vsb = kb.tinfo[self.VS.name][1]
        self.NCACHE = 6
        self.kc_sb = [sb(f"kc{i}", [128, PAST], BF16, at=ktb + i * 4096) for i in range(self.NCACHE)]
        self.vc_sb = [sb(f"vcs{i}", [128, 16, 128], BF16, at=vsb + i * 4096) for i in range(self.NCACHE)]
        self.KT_s = sb("KT_s", [128, 4, NS * TSP], BF16, at=ktb + 24576)
        self.dvs_tok = sb("dvs_tok", [TSP, NS, 512], BF16, at=ktb + 24576 + 1024)
        self.gvs_tok = sb("gvs_tok", [TSP, NS, 512], BF16, at=vsb + 24576)
        self.cache_loaded = 0
        self.pb = [kb.ps(f"pb{i}", [128, 512], F32) for i in range(8)]
        self.pa_i = 0
        self.pb_i = 0

        self.setup_consts()
        self.blk_ids = {}
        for i, sp_ in enumerate(self.tile_specs()):
            self.blk_ids[sp_] = i
        self.wbf = self.nc.dram_tensor("wbf16", [len(self.blk_ids), 128, 4096], BF16).ap()
        self.converted = set()
        self.conv_next = 0
        self.wq = []
        self.wq_pos = 0
        self.wq_issued = 0
        specs = []
        for t in range(self.n_tiles):
            specs += self.tile_specs()
        if self.do_sample:
            specs += self.tile_specs()
        import os
        st = int(os.environ.get("KSTAGE", "9"))
        if st < 9:
            one = self.tile_specs()
            ntile = self.n_tiles + (1 if self.do_sample else 0)
            if st == 1:
                one = one[:20]
            else:
                one = one[:28]
            specs = one * ntile
        self.wq = specs
        self.load_x(0, NTK)
        self.w_convert_ahead(8)
        for _ in range(self.NSLOT - 2):
            self.w_issue()
        self.ffn_prenorm(1, NTK, self.x2)
        tiles = [(t * NTK, NTK, True) for t in range(self.n_tiles)]
        if self.do_sample:
            tiles.append((T, NS * TSP, False))
        for i, (t0, n_, pr) in enumerate(tiles):
            nxt = (tiles[i + 1][0], tiles[i + 1][1]) if i + 1 < len(tiles) else None
            self.tile(t0, n_, prompt=pr, first=(i == 0), last=(pr and i == self.n_tiles - 1), nxt=nxt)
        kb.emit()

    def dbg(self, name, ap):
        import os
        if os.environ.get("KDBG", "0") != "1" or name in self.dbg_names:
            return
        self.dbg_names.add(name)
        d = self.nc.dram_tensor("dbg_" + name, list(ap.shape), F32, kind="ExternalOutput").ap()
        self.kb.dma("pool", d, ap, is_output=True)

    def tf(self):
        t = self.tfp[self.tfi % len(self.tfp)]
        self.tfi += 1
        return t

    def tb(self):
        t = self.tbp[self.tbi % len(self.tbp)]
        self.tbi += 1
        return t

    def psA(self):
        t = self.pb[self.pa_i % 4]
        self.pa_i += 1
        return t

    def psB(self):
        t = self.pb[4 + self.pb_i % 4]
        self.pb_i += 1
        return t

    def setup_consts(self):
        kb = self.kb
        kb.dma("sp", self.cst[:], self.cst_d)
        kb.dma("pool", self.ident[:], self.ident_d)
        kb.dma("pool", self.cmask[:], self.cmask_d)
        kb.dma("pool", self.wa2[:], self.w_a2)
        kb.memset("dve", self.ones_bf[:], 1.0)
        kb.memset("dve", self.ones_f, 1.0)
        kb.memset("dve", self.smask_p[:], 1.0)
        kb.memset("dve", self.smask_p[:].rearrange("p (c t) -> p c t", t=128)[:, :, 0:1], 0.0)
        kb.memset("dve", self.smask_s[:], 1.0)
        kb.memset("dve", self.smask_s[:].rearrange("p (c t) -> p c t", t=TSP)[:, :, 0:1], 0.0)
        kb.memset("dve", self.negv_s[:], 0.0)
        kb.memset("dve", self.negv_s[:].rearrange("p (c t) -> p c t", t=TSP)[:, :, 0:TS], -1.0 / 16.0)
        kb.memset("dve", self.S[:], 0.0)
        kb.memset("dve", self.S_bf[:], 0.0)
        dc = self.dcst
        c = self.cst
        kb.ts("dve", dc[:, 0:8], c[:, K_F1POST:K_F1POST + 8], 0.5, None, ALU.mult)
        kb.ts("dve", dc[:, 8:16], c[:, K_F2POST:K_F2POST + 8], 0.5, None, ALU.mult)
        kb.ts("dve", dc[:, 16:18], c[:, K_BA:K_BA + 2], -1.0, None, ALU.mult)
        kb.ts("dve", dc[:, 18:19], c[:, K_DIFFG:K_DIFFG + 1], 1.0 - LAM_INIT, None, ALU.mult)
        kb.tt("dve", dc[:, 20:21], c[:, K_LAM:K_LAM + 1], c[:, K_LAM + 1:K_LAM + 2], ALU.mult)
        kb.tt("dve", dc[:, 21:22], c[:, K_LAM + 2:K_LAM + 3], c[:, K_LAM + 3:K_LAM + 4], ALU.mult)
        p = self.psB()
        kb.mm(p[:, 0:2], self.ones_f, dc[:, 20:22])
        kb.act(dc[:, 22:24], p[:, 0:2], AF.Exp)
        kb.tt("dve", dc[:, 24:25], dc[:, 23:24], dc[:, 22:23], ALU.subtract)
        kb.ts("dve", dc[:, 19:20], dc[:, 24:25], -LAM_INIT, None, ALU.add)
        self.neg_lam = dc[:, 19:20]
        kb.dma("sp", self.wgr_f[:], self.w_gr)
        kb.copy("dve", self.wgr[:].rearrange("p c f -> p (c f)"), self.wgr_f[:])
        self.dg08 = dc[:, 18:19]

    def tile_specs(self):
        s = []
        for f in (1, 2):
            ff = []
            for jb in range(6):
                j0 = jb * 4
                j1 = min(NJ, j0 + 4)
                ff.append(("g", f, j0, j1))
                ff.append(("u", f, j0, j1))
            for mb in range(4):
                for jh in range(2):
                    ff.append(("d", f, mb, jh))
            if f == 1:
                s += ff
                s += [("in", C_GQ, 512), ("in", C_GV, 512), ("in", C_GG, 512), ("in", C_DQ, 512),
                      ("in", C_DK, 512), ("in", C_DV, 512), ("out", 0), ("out", 1)]
            else:
                s += ff
        return s

    def w_src_dst(self, spec, slot):
        wr = self.wring[slot]
        kind = spec[0]
        if kind in ("g", "u"):
            _, f, j0, j1 = spec
            w = self.fw[f][0 if kind == "g" else 1]
            n = (j1 - j0) * 128
            src = w.rearrange("(c p) f -> p c f", p=128)[:, :, j0 * 128:j1 * 128]
            dst = wr[:, 0:8 * n].rearrange("p (c f) -> p c f", c=8)
        elif kind == "d":
            _, f, mb, jh = spec
            w = self.fw[f][2]
            src = w.rearrange("(j p) m -> p j m", p=128)[:, jh * 11:(jh + 1) * 11, mb * 256:(mb + 1) * 256]
            dst = wr[:, 0:11 * 256].rearrange("p (j m) -> p j m", j=11)
        elif kind == "in":
            _, c0, n = spec
            src = self.w_in.rearrange("(c p) f -> p c f", p=128)[:, :, c0:c0 + n]
            dst = wr[:, 0:8 * n].rearrange("p (c f) -> p c f", c=8)
        elif kind == "out":
            _, mh = spec
            src = self.w_out.rearrange("(c p) f -> p c f", p=128)[:, :, mh * 512:(mh + 1) * 512]
            dst = wr[:, 0:8 * 512].rearrange("p (c f) -> p c f", c=8)
        return src, dst

    def w_issue(self):
        if self.wq_issued >= len(self.wq):
            return
        i = self.wq_issued
        spec = self.wq[i]
        src, dst = self.w_src_dst(spec, i % self.NSLOT)
        b = self.blk_ids[spec]
        n = 1
        for d_ in dst.shape[1:]:
            n *= d_
        sc3 = self.wbf[b][:, 0:n].rearrange("p (c f) -> p c f", c=dst.shape[1])
        tok = ("D", "wbf%d" % b)
        wr = self.wring[i % self.NSLOT]
        if b not in self.converted:
            self.converted.add(b)
            self.kb.dma("pool", dst, src)
            self.kb.dma("sp", self.wbf[b][:, 0:n], wr[:, 0:n], extra_writes=[tok])
        else:
            self.kb.dma("sp", wr[:, 0:n], self.wbf[b][:, 0:n], extra_reads=[tok])
        self.wq_issued += 1

    def w_convert_ahead(self, upto):
        return
        nb = len(self.blk_ids)
        while self.conv_next < min(upto, nb):
            spec = self.wq[self.conv_next]
            b = self.blk_ids[spec]
            src, dst = self.w_src_dst(spec, 0)
            n = 1
            for d_ in dst.shape[1:]:
                n *= d_
            sc3 = self.wbf[b][:, 0:n].rearrange("p (c f) -> p c f", c=dst.shape[1])
            self.converted.add(b)
            self.kb.dma("pool", sc3, src, extra_writes=[("D", "wbf%d" % b)])
            self.conv_next += 1

    def w_get(self, spec):
        i = self.wq_pos
        assert self.wq[i] == spec, (self.wq[i], spec)
        self.w_convert_ahead(i + 12)
        self.w_issue()
        _, dst = self.w_src_dst(spec, i % self.NSLOT)
        self.wq_pos += 1
        return dst

    def rms_rstd(self, chunks, N, dim, use_psA=False, presq=False):
        kb = self.kb
        n = len(chunks)
        if not presq:
            for c, ch in enumerate(chunks):
                kb.act(self.hb[:, c, 0:N], ch, AF.Square)
        p = self.psA() if use_psA else self.psB()
        for c in range(n):
            kb.mm(p[:, 0:N], self.ones_bf[:, :], self.hb[:, c, 0:N], start=(c == 0), stop=(c == n - 1))
        kb.act(self.rstd[:, 0:N], p[:, 0:N], AF.Ln, bias=EPS, scale=1.0 / dim)
        kb.act(self.rstd[:, 0:N], self.rstd[:, 0:N], AF.Exp, scale=-0.5)
        return self.rstd[:, 0:N]

    def norm_to_hb(self, src, N, gcol):
        kb = self.kb
        r = self.rms_rstd([src[:, c, 0:N] for c in range(8)], N, D)
        for c in range(8):
            g = self.cst[:, gcol + c:gcol + c + 1]
            if False:
                t = self.tf()
                kb.ts("pool", t[:, 0:N], src[:, c, 0:N], g, None, ALU.mult)
                kb.tt("pool", self.hb[:, c, 0:N], t[:, 0:N], r, ALU.mult)
            else:
                kb.stt("dve", self.hb[:, c, 0:N], src[:, c, 0:N], g, r, ALU.mult, ALU.mult)

    def residual_update(self, N, gtile, gcol, final_tok0=None, xsrc=None, mid_hook=None):
        kb = self.kb
        r = self.rms_rstd([self.f[:, c, 0:N] for c in range(8)], N, D, presq=True)
        if mid_hook is not None:
            mid_hook()
        if xsrc is None:
            xsrc = self.x
        for c in range(8):
            fc = self.f[:, c, 0:N]
            g = gtile[:, gcol + c:gcol + c + 1]
            dst = self.x[:, c, 0:N] if final_tok0 is None else fc
            if False:
                kb.ts("pool", fc, fc, g, None, ALU.mult)
                kb.tt("pool", fc, fc, r, ALU.mult)
                kb.tt("pool", dst, self.x[:, c, 0:N], fc, ALU.add)
            else:
                kb.stt("dve", fc, fc, g, r, ALU.mult, ALU.mult)
                kb.tt("dve", dst, xsrc[:, c, 0:N], fc, ALU.add)
            if final_tok0 is not None:
                kb.dma("sp", self.yT[c * 128:(c + 1) * 128, final_tok0:final_tok0 + N], fc, is_output=True)

    def ffn_prenorm(self, f, N, src):
        kb = self.kb
        pre = K_F1PRE if f == 1 else K_F2PRE
        SQ0 = NJ - 8
        for c in range(8):
            kb.act(self.hb[:, c, 0:N], src[:, c, 0:N], AF.Copy, scale=self.cst[:, pre + c:pre + c + 1])
            kb.act(self.a[:, SQ0 + c, 0:N], src[:, c, 0:N], AF.Square)
        p = self.psB()
        for c in range(8):
            kb.mm(p[:, 0:N], self.ones_bf[:, :], self.a[:, SQ0 + c, 0:N], start=(c == 0), stop=(c == 7))
        r2 = self.vtok_f[0]
        kb.act(r2[:, 0:N], p[:, 0:N], AF.Ln, bias=EPS, scale=1.0 / D)
        kb.act(r2[:, 0:N], r2[:, 0:N], AF.Exp, scale=-0.5)

    def ffn(self, f, N, xsrc=None, mid_hook=None):
        kb = self.kb
        r = self.vtok_f[0][:, 0:N]
        for jb in range(6):
            j0 = jb * 4
            j1 = min(NJ, j0 + 4)
            wg = self.w_get(("g", f, j0, j1))
            wu = self.w_get(("u", f, j0, j1))
            for j in range(j0, j1):
                jl = j - j0
                pg = self.psA()
                pu = self.psA()
                for k in range(8):
                    kb.mm(pg[:, 0:N], wg[:, k, jl * 128:(jl + 1) * 128], self.hb[:, k, 0:N], start=(k == 0), stop=(k == 7))
                for k in range(8):
                    kb.mm(pu[:, 0:N], wu[:, k, jl * 128:(jl + 1) * 128], self.hb[:, k, 0:N], start=(k == 0), stop=(k == 7))
                t1 = self.tf()
                kb.tt("dve", t1[:, 0:N], pg[:, 0:N], r, ALU.mult)
                sg = self.tb()
                kb.act(sg[:, 0:N], t1[:, 0:N], AF.Silu)
                t2 = self.tf()
                kb.tt("dve", t2[:, 0:N], pu[:, 0:N], r, ALU.mult)
                kb.tt("dve", self.a[:, j, 0:N], t2[:, 0:N], sg[:, 0:N], ALU.mult)
        for mb in range(4):
            p0 = self.psB()
            p1 = self.psB()
            pp = (p0, p1)
            for jh in range(2):
                wd = self.w_get(("d", f, mb, jh))
                for ml in range(2):
                    for jl in range(11):
                        j = jh * 11 + jl
                        kb.mm(pp[ml][:, 0:N], wd[:, jl, ml * 128:(ml + 1) * 128], self.a[:, j, 0:N],
                              start=(j == 0), stop=(j == NJ - 1))
            for ml in range(2):
                kb.act(self.hb[:, mb * 2 + ml, 0:N], pp[ml][:, 0:N], AF.Square)
                kb.copy("act", self.f[:, mb * 2 + ml, 0:N], pp[ml][:, 0:N])
        self.dbg(f"a{f}", self.a[:, :, 0:N])
        self.dbg(f"f{f}", self.f[:, :, 0:N])
        self.residual_update(N, self.dcst, 0 if f == 1 else 8, final_tok0=(self.cur_tok0 if f == 2 else None),
                             xsrc=xsrc, mid_hook=mid_hook)
        self.dbg(f"rstdf{f}", self.rstd[:, 0:N])

    def proj_fm(self, wblk, col0, N, use_psB=False):
        kb = self.kb
        p = self.psB() if use_psB else self.psA()
        for k in range(8):
            kb.mm(p[:, 0:N], wblk[:, k, col0:col0 + 128], self.hb[:, k, 0:N], start=(k == 0), stop=(k == 7))
        return p

    def mixer(self, tok0, N, prompt):
        kb = self.kb
        L = 128 if prompt else TSP
        nchunk = N // L
        self.norm_to_hb(self.x, N, K_MPRE)
        wgr = self.wgr
        p = self.psA()
        for k in range(8):
            kb.mm(p[0:32, 0:N], wgr[:, k, 0:32], self.hb[:, k, 0:N], start=(k == 0), stop=(k == 7))
        kb.copy("act", self.gr_bf[0:32, 0:N], p[0:32, 0:N])
        import os
        ksub = int(os.environ.get("KSUB", "99"))
        if ksub < 2:
            return
        wqk = self.w_get(("in", C_GQ, 512))
        smask = self.smask_p if prompt else self.smask_s
        qv = self.qt[:, :, 0:N].rearrange("p (hp e) n -> p hp e n", e=2)
        kb.memset("pool", qv[64:128, :, 0, :], 0.0)
        kb.memset("pool", qv[0:64, :, 1, :], 0.0)
        eL = []
        for hp in range(2):
            p = self.psA()
            kb.mm(p[:, 0:N], self.wa2[0:16, hp * 128:(hp + 1) * 128], self.gr_bf[0:16, 0:N])
            e = self.tf()
            kb.act(e[:, 0:N], p[:, 0:N], AF.Exp, bias=self.dcst[:, 16 + hp:17 + hp], scale=-1.0)
            kb.act(e[:, 0:N], e[:, 0:N], AF.Ln, bias=1.0)
            la = e
            if prompt:
                kb.ts("dve", la[:, 0:N], e[:, 0:N], -1.0 / 16.0, None, ALU.mult)
            else:
                kb.tt("dve", la[:, 0:N], e[:, 0:N], self.negv_s[:, 0:N], ALU.mult)
            b = self.tf()
            kb.scan(b[:, 0:N], smask[:, 0:N], la[:, 0:N], 0.0, ALU.mult, ALU.add)
            Eb = self.tf()
            kb.act(Eb[:, 0:N], b[:, 0:N], AF.Exp)
            Enb = self.tf()
            kb.act(Enb[:, 0:N], b[:, 0:N], AF.Exp, scale=-1.0)
            pq = self.proj_fm(wqk, hp * 128, N)
            kb.stt("dve", self.qt[0:64, 2 * hp, 0:N], pq[0:64, 0:N], 0.125, Eb[0:64, 0:N], ALU.mult, ALU.mult)
            kb.stt("dve", self.qt[64:128, 2 * hp + 1, 0:N], pq[64:128, 0:N], 0.125, Eb[64:128, 0:N], ALU.mult, ALU.mult)
            pk = self.proj_fm(wqk, 256 + hp * 128, N)
            kb.tt("dve", self.kt[:, hp, 0:N], pk[:, 0:N], Enb[:, 0:N], ALU.mult)
            eLt = self.eL[hp]
            for c in range(nchunk):
                kb.copy("dve", eLt[:, c:c + 1], Eb[:, c * L + L - 1:c * L + L])
            eL.append(eLt)
        if ksub < 3:
            return
        wgv = self.w_get(("in", C_GV, 512))
        for c in range(nchunk):
            p = self.psA()
            for k in range(8):
                kb.mm(p[0:L, :], self.hb[:, k, c * L:(c + 1) * L], wgv[:, k, :], start=(k == 0), stop=(k == 7))
            dst = self.gv_tok[0:L, c, :] if prompt else self.gvs_tok[0:L, c, :]
            kb.copy("act", dst, p[0:L, :])

        wcache = {}

        def wblk(key, spec):
            if key not in wcache:
                wcache[key] = self.w_get(spec)
            return wcache[key]

        def unit_gg(h):
            w = wblk("gg", ("in", C_GG, 512))
            p = self.proj_fm(w, h * 128, N, use_psB=True)
            kb.act(self.gg_bf[:, h, 0:N], p[:, 0:N], AF.Silu)

        def unit_dq(h):
            w = wblk("dq", ("in", C_DQ, 512))
            if h == 0:
                kb.memset("pool", self.dq1[64:128, :, 0:N], 0.0)
                kb.memset("pool", self.dq2[0:64, :, 0:N], 0.0)
            p = self.proj_fm(w, h * 128, N, use_psB=True)
            kb.copy("act", self.dq1[0:64, h, 0:N], p[0:64, 0:N])
            kb.copy("act", self.dq2[64:128, h, 0:N], p[64:128, 0:N])

        def unit_dk(h):
            w = wblk("dk", ("in", C_DK, 512))
            p = self.proj_fm(w, h * 128, N, use_psB=True)
            t = self.tf()
            kb.copy("act", t[:, 0:N], p[:, 0:N])
            kb.dma("sp", self.kT_out[h * 128:(h + 1) * 128, tok0:tok0 + N], t[:, 0:N], is_output=True)
            if prompt:
                kb.copy("dve", self.KT[:, h, tok0:tok0 + N], p[:, 0:N])
            else:
                kb.copy("dve", self.KT_s[:, h, 0:N], p[:, 0:N])

        def unit_dv(c):
            w = wblk("dv", ("in", C_DV, 512))
            p = self.psB()
            for k in range(8):
                kb.mm(p[0:L, :], self.hb[:, k, c * L:(c + 1) * L], w[:, k, :], start=(k == 0), stop=(k == 7))
            t = self.vtok_f[self.vtf_i % 1]
            self.vtf_i += 1
            kb.copy("act", t[0:L, :], p[0:L, :])
            kb.dma("sp", self.v_out[tok0 + c * L:tok0 + (c + 1) * L, :], t[0:L, :], is_output=True)
            if prompt:
                kb.copy("dve", self.VS[:, (tok0 // 128) + c, :], p[:, :])
            else:
                kb.copy("dve", self.dvs_tok[0:L, c, :], p[0:L, :])

        units = [(unit_gg, i) for i in range(4)] + [(unit_dq, i) for i in range(4)] + \
                [(unit_dk, i) for i in range(4)] + [(unit_dv, i) for i in range(nchunk)]
        if ksub < 7:
            return
        if self.stage < 4:
            kb.memset("dve", self.merge[:, 4:8, :], 0.0)
        self.gla(tok0, N, L, nchunk, prompt, eL, units)
        if self.stage >= 4:
            if prompt:
                last_finish = self.attn_prompt(tok0, N)
            else:
                kb.memset("pool", self.merge[:, 4:8, 0:N], 0.0)
                last_finish = self.attn_sample(N)
        else:
            last_finish = (lambda: None)
        wo = self.w_get(("out", 0))
        banks = [self.pb[0], self.pb[1], self.pb[2], self.pb[3]]
        for k in range(7):
            for ml in range(4):
                kb.mm(banks[ml][:, 0:N], wo[:, k, ml * 128:(ml + 1) * 128], self.merge[:, k, 0:N], start=(k == 0), stop=False)
        last_finish()
        for ml in range(4):
            kb.mm(banks[ml][:, 0:N], wo[:, 7, ml * 128:(ml + 1) * 128], self.merge[:, 7, 0:N], start=False, stop=True)
            kb.act(self.hb[:, ml, 0:N], banks[ml][:, 0:N], AF.Square)
            kb.copy("act", self.f[:, ml, 0:N], banks[ml][:, 0:N])
        wo = self.w_get(("out", 1))
        for ml in range(4):
            p = self.psB()
            for k in range(8):
                kb.mm(p[:, 0:N], wo[:, k, ml * 128:(ml + 1) * 128], self.merge[:, k, 0:N], start=(k == 0), stop=(k == 7))
            kb.act(self.hb[:, 4 + ml, 0:N], p[:, 0:N], AF.Square)
            kb.copy("act", self.f[:, 4 + ml, 0:N], p[:, 0:N])
        self.residual_update(N, self.cst, K_MPOST)

    def gla(self, tok0, N, L, nchunk, prompt, eL, units=()):
        kb = self.kb
        units = list(units)
        gv = self.gv_tok if prompt else self.gvs_tok

        def stage1(c):
            cols = slice(c * L, (c + 1) * L)
            pt = self.psA()
            ptb = pt[:].bitcast(BF16)
            for hp in range(2):
                kb.transpose(ptb[0:L, hp * 128:(hp + 1) * 128], self.kt[:, hp, cols], self.ident[:, :])
            ktk = self.tb()
            kb.copy("dve", ktk[0:L, 0:256], ptb[0:L, 0:256])
            pab = (self.psA(), self.psA())
            for h in range(4):
                hp = h // 2
                rows = slice((h % 2) * 64, (h % 2) * 64 + 64)
                kb.mm(pab[h % 2][0:L, h * 128:h * 128 + L], self.kt[:, hp, cols], self.qt[:, h, cols])
            at = self.tb()
            for h in range(4):
                kb.tt("dve", at[0:L, h * 128:h * 128 + L], pab[h % 2][0:L, h * 128:h * 128 + L], self.cmask[0:L, 0:L], ALU.mult)
            return at, ktk

        def stage2(c, at, ktk):
            cols = slice(c * L, (c + 1) * L)
            if not prompt:
                kb.dma("sp", self.S[:], self.state_s[c])
                kb.copy("dve", self.S_bf[:], self.S[:])
            po = self.psA()
            pS = self.psA()
            for h in range(4):
                hp = h // 2
                rows = slice((h % 2) * 64, (h % 2) * 64 + 64)
                kb.mm(po[:, h * 128:h * 128 + L], gv[0:L, c, h * 128:(h + 1) * 128], at[0:L, h * 128:h * 128 + L],
                      start=True, stop=False)
                kb.mm(po[:, h * 128:h * 128 + L], self.S_bf[:, hp, :], self.qt[:, h, cols], start=False, stop=True)
            for h in range(4):
                hp = h // 2
                kb.mm(pS[:, h * 128:(h + 1) * 128], ktk[0:L, hp * 128:(hp + 1) * 128], gv[0:L, c, h * 128:(h + 1) * 128])
            for h in range(4):
                kb.copy("act", self.o_sb[:, h, cols], po[:, h * 128:h * 128 + L])
                kb.act(self.f[:, h, :].bitcast(BF16)[:, cols], po[:, h * 128:h * 128 + L], AF.Square)
            t = self.tf()
            pSv = pS[:, 0:512].rearrange("p (hp e v) -> p hp e v", hp=2, e=2)
            tv = t[:, 0:256].rearrange("p (hp v) -> p hp v", hp=2)
            kb.tt("dve", tv[0:64, :, :], pSv[0:64, :, 0, :], self.S[0:64, :, :], ALU.add)
            kb.tt("dve", tv[64:128, :, :], pSv[64:128, :, 1, :], self.S[64:128, :, :], ALU.add)
            for hp in range(2):
                kb.ts("dve", self.S[:, hp, :], tv[:, hp, :], eL[hp][:, c:c + 1], None, ALU.mult)
            kb.copy("dve", self.S_bf[:], self.S[:])
            if not prompt:
                kb.dma("sp", self.S_s[c], self.S[:], is_output=True)

        cur = stage1(0)
        for c in range(nchunk):
            nxt = stage1(c + 1) if c + 1 < nchunk else None
            stage2(c, *cur)
            for _ in range(4):
                if units:
                    fn, arg = units.pop(0)
                    fn(arg)
            cur = nxt
        while units:
            fn, arg = units.pop(0)
            fn(arg)
        pss = [self.psB() for h in range(4)]
        rts = [self.tf() for h in range(4)]
        for h in range(4):
            kb.mm(pss[h][:, 0:N], self.ones_bf[:, :], self.f[:, h, :].bitcast(BF16)[:, 0:N])
        for h in range(4):
            kb.act(rts[h][:, 0:N], pss[h][:, 0:N], AF.Ln, bias=EPS, scale=1.0 / 128)
        for h in range(4):
            kb.act(rts[h][:, 0:N], rts[h][:, 0:N], AF.Exp, scale=-0.5)
        for h in range(4):
            oh = self.o_sb[:, h, 0:N]
            kb.stt("dve", oh, oh, self.cst[:, K_GLAG:K_GLAG + 1], rts[h][:, 0:N], ALU.mult, ALU.mult)
            kb.tt("dve", self.merge[:, h, 0:N], oh, self.gg_bf[:, h, 0:N], ALU.mult)

    def attn_core(self, N, h, q_aps, ktiles, hook=None, G=1):
        kb = self.kb
        O = (self.pb[4], self.pb[5])
        Ls = (self.pb[6], self.pb[7])
        nt = len(ktiles)
        groups = []
        j = 0
        while j < nt:
            if G > 1 and ktiles[j][2] == 128 and ktiles[j][3] == 128 and not ktiles[j][5]:
                g = [j]
                while len(g) < G and g[-1] + 1 < nt and ktiles[g[-1] + 1][2] == 128 and ktiles[g[-1] + 1][3] == 128 \
                        and not ktiles[g[-1] + 1][5]:
                    g.append(g[-1] + 1)
                groups.append(g)
                j = g[-1] + 1
            else:
                groups.append([j])
                j += 1

        def qk(grp):
            pss = []
            for c in range(2):
                ps = self.psA()
                for gi, j in enumerate(grp):
                    KTa, Va, nkM, nk, q0, diag = ktiles[j]
                    if len(grp) == 1:
                        kb.mm(ps[0:nkM, q0:N], KTa, q_aps[c][:, q0:N])
                    else:
                        kb.mm(ps[0:nkM, gi * N:(gi + 1) * N], KTa, q_aps[c][:, 0:N])
                pss.append(ps)
            return pss

        def rest(grp, pss):
            for c in range(2):
                ps = pss[c]
                P = self.tb()
                if len(grp) == 1:
                    j = grp[0]
                    KTa, Va, nkM, nk, q0, diag = ktiles[j]
                    kb.act(P[0:nk, q0:N], ps[0:nk, q0:N], AF.Exp, scale=0.125)
                    if diag:
                        kb.memset("dve", P[64:128, q0:q0 + 64], 0.0)
                    kb.mm(O[c][:, q0:N], Va, P[0:nk, q0:N], start=(j == 0), stop=(j == nt - 1))
                    kb.mm(Ls[c][:, q0:N], self.ones_bf[0:nk, :], P[0:nk, q0:N], start=(j == 0), stop=(j == nt - 1))
                else:
                    W = len(grp) * N
                    kb.act(P[:, 0:W], ps[:, 0:W], AF.Exp, scale=0.125)
                    for gi, j in enumerate(grp):
                        Va = ktiles[j][1]
                        kb.mm(O[c][:, 0:N], Va, P[:, gi * N:(gi + 1) * N], start=(j == 0), stop=(j == nt - 1))
                    for gi, j in enumerate(grp):
                        kb.mm(Ls[c][:, 0:N], self.ones_bf[:, :], P[:, gi * N:(gi + 1) * N], start=(j == 0), stop=(j == nt - 1))

        prev = None
        for gidx, grp in enumerate(groups):
            pss = qk(grp)
            if prev is not None:
                rest(*prev)
            prev = (grp, pss)
            if hook is not None and gidx == min(2, len(groups) - 1):
                hook()
                hook = None
        rest(*prev)
        if hook is not None:
            hook()
        r1 = self.tf()
        r2 = self.tf()
        t1 = self.tf()
        t2 = self.tf()
        kb.recip(r1[:, 0:N], Ls[0][:, 0:N])
        kb.copy("act", t1[:, 0:N], O[0][:, 0:N])
        kb.recip(r2[:, 0:N], Ls[1][:, 0:N])
        kb.copy("act", t2[:, 0:N], O[1][:, 0:N])
        return (r1, r2, t1, t2)

    def attn_finishA(self, ev, N, h):
        kb = self.kb
        r1, r2, t1, t2 = ev
        kb.tt("dve", t1[:, 0:N], t1[:, 0:N], r1[:, 0:N], ALU.mult)
        kb.tt("dve", t2[:, 0:N], t2[:, 0:N], r2[:, 0:N], ALU.mult)
        od = self.o_sb[:, h, 0:N]
        kb.stt("dve", od, t2[:, 0:N], self.neg_lam, t1[:, 0:N], ALU.mult, ALU.add)
        kb.act(self.hb[:, 0, 0:N], od, AF.Square)

    def attn_finishB(self, N, h, cols=None, bank=None):
        kb = self.kb
        od = self.o_sb[:, h, 0:N]
        if bank is not None:
            p = bank
        else:
            p = self.psA()
            if self.pa_i % 2 == 1:
                self.pa_i += 1
        kb.mm(p[:, 0:N], self.ones_bf[:, :], self.hb[:, 0, 0:N])
        kb.act(self.rstd[:, 0:N], p[:, 0:N], AF.Ln, bias=EPS, scale=1.0 / 128)
        kb.act(self.rstd[:, 0:N], self.rstd[:, 0:N], AF.Exp, scale=-0.5)
        dst = self.merge[:, 4 + h, 0:N] if cols is None else self.merge[:, 4 + h, cols]
        kb.stt("dve", dst, od, self.dg08, self.rstd[:, 0:N], ALU.mult, ALU.mult)

    def attn_prompt(self, tok0, N):
        i = tok0 // NTK
        pending = None
        for h in range(4):
            ktiles = []
            for j in range(4 * i + 4):
                m = j - 4 * i
                q0 = 0 if m < 0 else 128 * m
                KTa = self.KT[:, h, j * 128:(j + 1) * 128]
                Va = self.VS[:, j, h * 128:(h + 1) * 128]
                ktiles.append((KTa, Va, 128, 128, q0, m >= 0))
            hook = None
            if pending is not None:
                pe, ph = pending
                hook = (lambda pe=pe, ph=ph: self.attn_finishA(pe, N, ph))
            ev = self.attn_core(N, h, (self.dq1[:, h, 0:N], self.dq2[:, h, 0:N]), ktiles, hook)
            if pending is not None:
                self.attn_finishB(N, pending[1])
            pending = (ev, h)
        return (lambda: (self.attn_finishA(pending[0], N, pending[1]), self.attn_finishB(N, pending[1], bank=self.pb[4])))

    def cache_prefetch(self, upto):
        while self.cache_loaded <= min(upto, 4 * NS - 1):
            i = self.cache_loaded
            h, s_ = i // NS, i % NS
            self.kb.dma("pool", self.kc_sb[i % self.NCACHE][:], self.kTc[s_, h])
            self.kb.dma("pool", self.vc_sb[i % self.NCACHE][:], self.vc[s_, :, h, :].rearrange("(j p) v -> p j v", p=128))
            self.cache_loaded += 1

    def attn_sample(self, N):
        kb = self.kb
        pending = None
        for h in range(4):
            for s in range(NS):
                i = self.cache_i
                self.cache_i += 1
                kc = self.kc_sb[i % self.NCACHE]
                vcs = self.vc_sb[i % self.NCACHE]
                self.cache_prefetch(i + self.NCACHE - 1)
                cols = slice(s * TSP, s * TSP + TS)
                colsM = slice(s * TSP, (s + 1) * TSP)
                ktiles = []
                for j in range(PAST // 128):
                    ktiles.append((kc[:, j * 128:(j + 1) * 128], vcs[:, j, :], 128, 128, 0, False))
                ktiles.append((self.KT_s[:, h, colsM], self.dvs_tok[0:TS, s, h * 128:(h + 1) * 128], TSP, TS, 0, False))
                hook = None
                if pending is not None:
                    pe, ph, pc = pending
                    hook = (lambda pe=pe, ph=ph: self.attn_finishA(pe, TS, ph))
                ev = self.attn_core(TS, h, (self.dq1[:, h, cols], self.dq2[:, h, cols]), ktiles, hook, G=16)
                if pending is not None:
                    self.attn_finishB(TS, pending[1], pending[2])
                pending = (ev, h, cols)
        return (lambda: (self.attn_finishA(pending[0], TS, pending[1]), self.attn_finishB(TS, pending[1], pending[2], bank=self.pb[4])))

    def load_x(self, tok0, N):
        for c in range(8):
            self.kb.dma("pool", self.x2[:, c, 0:N], self.xT[c * 128:(c + 1) * 128, tok0:tok0 + N])

    def tile(self, tok0, N, prompt, first=False, last=False, nxt=None):
        kb = self.kb
        self.cur_tok0 = tok0
        self.stage = 9
        if not prompt:
            self.cache_prefetch(self.NCACHE - 2)
        self.ffn(1, N, xsrc=self.x2)
        self.mixer(tok0, N, prompt)
        if nxt is not None:
            self.load_x(nxt[0], nxt[1])
        self.ffn_prenorm(2, N, self.x)
        hook = None
        if nxt is not None:
            hook = (lambda: self.ffn_prenorm(1, nxt[1], self.x2))
        self.ffn(2, N, xsrc=self.x, mid_hook=hook)
        if prompt and last:
            kb.dma("sp", self.S_p, self.S[:], is_output=True)


_PROG_CACHE = {}


def _get_prog(n_tiles=8, do_sample=True):
    key = (n_tiles, do_sample)
    if key not in _PROG_CACHE:
        nc = bass.Bass("TRN2", target_bir_lowering=False)
        Prog(nc, n_tiles=n_tiles, do_sample=do_sample)
        _PROG_CACHE[key] = nc
    return _PROG_CACHE[key]


def _lay_wgr(w_in):
    g = np.zeros((128, 8, 32), np.float32)
    g[:, :, 0:16] = np.asarray(w_in[:, C_GR:C_GR + 16], np.float32).reshape(8, 128, 16).transpose(1, 0, 2)
    return np.ascontiguousarray(g.reshape(128, 256))


def _pack_consts(inp):
    c = np.zeros((128, NCST), np.float32)
    def g8(v):
        return np.ascontiguousarray(np.asarray(v, np.float32).reshape(8, 128).T)
    c[:, K_F1PRE:K_F1PRE + 8] = g8(inp["ffn1_pre_g"][0])
    c[:, K_F1POST:K_F1POST + 8] = g8(inp["ffn1_post_g"][0])
    c[:, K_MPRE:K_MPRE + 8] = g8(inp["mix_pre_g"][0])
    c[:, K_MPOST:K_MPOST + 8] = g8(inp["mix_post_g"][0])
    c[:, K_F2PRE:K_F2PRE + 8] = g8(inp["ffn2_pre_g"][0])
    c[:, K_F2POST:K_F2POST + 8] = g8(inp["ffn2_post_g"][0])
    c[:, K_BA:K_BA + 2] = np.asarray(inp["b_gate_a"][0], np.float32).reshape(2, 128).T
    c[:, K_GLAG] = inp["gla_norm_g"][0]
    c[:, K_DIFFG] = inp["diff_norm_g"][0]
    for i, k in enumerate(("lambda_q1", "lambda_k1", "lambda_q2", "lambda_k2")):
        c[0:64, K_LAM + i] = inp[k][0]
    return c


def _run(inp, n_tiles=8, do_sample=True):
    import os
    do_sample = do_sample and os.environ.get("KNOSAMPLE", "0") != "1"
    nc = _get_prog(n_tiles, do_sample)
    T = n_tiles * NTK
    f32 = lambda a: np.ascontiguousarray(np.asarray(a, np.float32))
    cst = _pack_consts(inp)
    ident = np.eye(128, dtype=np.float32)
    cmask = np.triu(np.ones((128, 128), np.float32))
    shared = {
        "cst": cst, "w_in": f32(inp["w_in"][0]), "w_a2": f32(inp["w_gate_a2"][0]), "w_out": f32(inp["w_out"][0]),
        "f1g": f32(inp["ffn1_w_gate"][0]), "f1u": f32(inp["ffn1_w_up"][0]), "f1d": f32(inp["ffn1_w_down"][0]),
        "f2g": f32(inp["ffn2_w_gate"][0]), "f2u": f32(inp["ffn2_w_up"][0]), "f2d": f32(inp["ffn2_w_down"][0]),
        "ident": ident, "cmask": cmask,
        "w_gr": _lay_wgr(inp["w_in"][0]),
    }
    in_maps = []
    for c in range(8):
        xp = np.asarray(inp["x_prompt"][c][:T], np.float32).T
        xs = np.zeros((NS, TSP, D), np.float32)
        xs[:, :TS] = np.asarray(inp["x_sample"][NS * c:NS * c + NS], np.float32)
        xs = xs.reshape(NS * TSP, D).T
        xT = np.ascontiguousarray(np.concatenate([xp, xs], axis=1)) if do_sample else np.ascontiguousarray(xp)
        if not do_sample:
            m = dict(shared)
            m["xT"] = xT
            in_maps.append(m)
            continue
        st = np.asarray(inp["state_gla"][0, NS * c:NS * c + NS], np.float32)
        st = st.reshape(NS, 2, 2, 64, 128).transpose(0, 2, 3, 1, 4).reshape(NS, 128, 2, 128)
        kTc = np.asarray(inp["cache_diff_k"][0, NS * c:NS * c + NS], np.float32).transpose(0, 2, 3, 1)
        vc = np.asarray(inp["cache_diff_v"][0, NS * c:NS * c + NS], np.float32)
        m = dict(shared)
        m.update({"xT": xT, "state_s": f32(st), "kTc": f32(kTc), "vc": f32(vc)})
        in_maps.append(m)
    res = run_bass_kernel_spmd(nc, in_maps, core_ids=list(range(8)))
    if not do_sample:
        for r in res.results:
            r["yT"] = np.concatenate([r["yT"], np.zeros((D, NS * TSP), np.float32)], axis=1)
            r["kT_out"] = np.concatenate([r["kT_out"], np.zeros((512, NS * TSP), np.float32)], axis=1)
            r["v_out"] = np.concatenate([r["v_out"], np.zeros((NS * TSP, 512), np.float32)], axis=0)
            r["S_s"] = np.zeros((NS, 128, 2, 128), np.float32)
    return res.results, T


def _assemble(results, T):
    y_p = np.zeros((8, T, D), np.float32)
    y_s = np.zeros((8 * NS, TS, D), np.float32)
    sg_p = np.zeros((1, 8, 4, 64, 128), np.float32)
    k_p = np.zeros((1, 8, T, 4, 128), np.float32)
    v_p = np.zeros((1, 8, T, 4, 128), np.float32)
    sg_s = np.zeros((1, 8 * NS, 4, 64, 128), np.float32)
    k_s = np.zeros((1, 8 * NS, TS, 4, 128), np.float32)
    v_s = np.zeros((1, 8 * NS, TS, 4, 128), np.float32)
    unS = lambda s: s.reshape(2, 64, 2, 128).transpose(2, 0, 1, 3).reshape(4, 64, 128)
    for c in range(8):
        r = results[c]
        yT = r["yT"]
        y_p[c] = yT[:, :T].T
        y_s[NS * c:NS * c + NS] = yT[:, T:].T.reshape(NS, TSP, D)[:, :TS]
        kT = r["kT_out"]
        k_p[0, c] = kT[:, :T].T.reshape(T, 4, 128)
        k_s[0, NS * c:NS * c + NS] = kT[:, T:].T.reshape(NS, TSP, 4, 128)[:, :TS]
        vo = r["v_out"]
        v_p[0, c] = vo[:T].reshape(T, 4, 128)
        v_s[0, NS * c:NS * c + NS] = vo[T:].reshape(NS, TSP, 4, 128)[:, :TS]
        sg_p[0, c] = unS(r["S_p"])
        for s in range(NS):
            sg_s[0, NS * c + s] = unS(r["S_s"][s])
    return (y_p, y_s, sg_p, k_p, v_p, sg_s, k_s, v_s)


def kernel(**inputs):
    results, T = _run(inputs, 8, True)
    return _assemble(results, T)
```
